# Optimizing a Trainium2 kernel written in Bass

```python
import jax, jax.numpy as jnp
from jax import lax
import numpy as np

D_MODEL = 1024
BATCH = 8
SEQ = 2048
DEPTH = 4
DEC_BATCH = 128
DEC_SEQ = 4
PAST_LEN = 8192
PAGE_SIZE = 128

HEAD_DIM = 64
N_Q_HEADS = 8
N_KV_HEADS = 2
Q_PER_KV = N_Q_HEADS // N_KV_HEADS
D_ATTN = N_Q_HEADS * HEAD_DIM
D_KV = N_KV_HEADS * HEAD_DIM
D_CONV = D_MODEL - D_ATTN
D_MIX = D_ATTN + D_CONV
IN_W = D_ATTN + 2 * D_KV + 2 * D_CONV
WINDOW = 128
Q_BLOCK = 128
CONV_K = 31
CONV_HIST = CONV_K - 1
D_FF = 4 * D_MODEL
D_PLE = 256
ROPE_THETA = 10000.0
EPS = 1e-6
NEG = -1e30

kernel_name = 'swa_sink_conformer_hybrid'


def rmsnorm(x, g):
    xf = x.astype(jnp.float32)
    y = xf * lax.rsqrt(jnp.mean(xf * xf, axis=-1, keepdims=True) + EPS) * g.astype(jnp.float32)
    return y.astype(x.dtype)


def layernorm(x, g, b):
    xf = x.astype(jnp.float32)
    mu = jnp.mean(xf, axis=-1, keepdims=True)
    xc = xf - mu
    var = jnp.mean(xc * xc, axis=-1, keepdims=True)
    y = xc * lax.rsqrt(var + EPS) * g.astype(jnp.float32) + b.astype(jnp.float32)
    return y.astype(x.dtype)


def rope(x, pos):
    half = HEAD_DIM // 2
    inv = 1.0 / (ROPE_THETA ** (jnp.arange(half, dtype=jnp.float32) / half))
    ang = pos.astype(jnp.float32)[:, None] * inv[None, :]
    cos = jnp.cos(ang)[None, :, None, :]
    sin = jnp.sin(ang)[None, :, None, :]
    xf = x.astype(jnp.float32)
    x1, x2 = xf[..., :half], xf[..., half:]
    return jnp.concatenate([x1 * cos - x2 * sin, x2 * cos + x1 * sin], axis=-1).astype(x.dtype)


def window_attention(q, k_ext, v_ext, sinks, pos0):
    B, T = q.shape[0], q.shape[1]
    qb = Q_BLOCK if T % Q_BLOCK == 0 else T
    nb = T // qb
    L = WINDOW + qb
    starts = jnp.arange(nb, dtype=jnp.int32) * qb
    idx = starts[:, None] + jnp.arange(L, dtype=jnp.int32)[None, :]
    kb = k_ext[:, idx]
    vb = v_ext[:, idx]
    qr = q.reshape(B, nb, qb, N_KV_HEADS, Q_PER_KV, HEAD_DIM)
    s = jnp.einsum('bnqkgd,bnlkd->bnkgql', qr, kb).astype(jnp.float32) * (HEAD_DIM ** -0.5)
    qpos = pos0 + starts[:, None] + jnp.arange(qb, dtype=jnp.int32)[None, :]
    kpos = pos0 - WINDOW + idx
    dist = qpos[:, :, None] - kpos[:, None, :]
    mask = (kpos[:, None, :] >= 0) & (dist >= 0) & (dist <= WINDOW)
    s = jnp.where(mask[None, :, None, None], s, NEG)
    sink = sinks.astype(jnp.float32).reshape(N_KV_HEADS, Q_PER_KV)[None, None, :, :, None, None]
    sink_col = jnp.broadcast_to(sink, s.shape[:-1] + (1,))
    pr = jax.nn.softmax(jnp.concatenate([s, sink_col], axis=-1), axis=-1)[..., :-1]
    o = jnp.einsum('bnkgql,bnlkd->bnqkgd', pr.astype(v_ext.dtype), vb)
    return o.reshape(B, T, D_ATTN)


def causal_depthwise(u_ext, w, b):
    out = lax.conv_general_dilated(u_ext, w[:, None, :], window_strides=(1,), padding='VALID',
                                   dimension_numbers=('NWC', 'WIO', 'NWC'), feature_group_count=D_CONV)
    return out + b


def layer(x, p, hist_k, hist_v, hist_u, pos0, g_mix, w_in, sinks, w_dw, b_dw, ln_g, ln_b, w_pw2,
          g_attn_out, g_conv_out, w_o, g_ffn, w_up, w_down, g_ple, w_ple_gate, w_ple):
    B, T = x.shape[0], x.shape[1]
    h = rmsnorm(x, g_mix)
    z = h @ w_in
    q, k, v, u_a, u_g = jnp.split(z, [D_ATTN, D_ATTN + D_KV, D_ATTN + 2 * D_KV, D_ATTN + 2 * D_KV + D_CONV], axis=-1)
    pos = pos0 + jnp.arange(T, dtype=jnp.int32)
    q = rope(q.reshape(B, T, N_Q_HEADS, HEAD_DIM), pos)
    k = rope(k.reshape(B, T, N_KV_HEADS, HEAD_DIM), pos)
    v = v.reshape(B, T, N_KV_HEADS, HEAD_DIM)
    k_ext = jnp.concatenate([hist_k.astype(k.dtype), k], axis=1)
    v_ext = jnp.concatenate([hist_v.astype(v.dtype), v], axis=1)
    a = window_attention(q, k_ext, v_ext, sinks, pos0)
    u = u_a * jax.nn.sigmoid(u_g)
    u_ext = jnp.concatenate([hist_u.astype(u.dtype), u], axis=1)
    c = causal_depthwise(u_ext, w_dw, b_dw)
    c = jax.nn.silu(layernorm(c, ln_g, ln_b)) @ w_pw2
    mix = jnp.concatenate([rmsnorm(a, g_attn_out), rmsnorm(c, g_conv_out)], axis=-1) @ w_o
    x = x + mix
    x = x + jnp.square(jax.nn.relu(rmsnorm(x, g_ffn) @ w_up)) @ w_down
    x = x + (p @ w_ple) * jax.nn.sigmoid(rmsnorm(x, g_ple) @ w_ple_gate)
    return x, k_ext[:, -WINDOW:], v_ext[:, -WINDOW:], u_ext[:, -CONV_HIST:]


def setup_inputs(seed: int = 0) -> dict:
    key = jax.random.key(seed)
    ks = jax.random.split(key, 24)
    f32 = jnp.float32

    def nrm(k, shape, scale):
        return jax.random.normal(k, shape, f32) * scale

    def gain(k, shape):
        return 1.0 + 0.02 * jax.random.normal(k, shape, f32)

    return {
        'x_prompt': nrm(ks[0], (BATCH, SEQ, D_MODEL), 1.0),
        'x_sample': nrm(ks[1], (DEC_BATCH, DEC_SEQ, D_MODEL), 1.0),
        'p_prompt': nrm(ks[2], (DEPTH, BATCH, SEQ, D_PLE), 1.0),
        'p_sample': nrm(ks[3], (DEPTH, DEC_BATCH, DEC_SEQ, D_PLE), 1.0),
        'cache_k': nrm(ks[4], (DEPTH, DEC_BATCH, WINDOW, N_KV_HEADS, HEAD_DIM), 1.0),
        'cache_v': nrm(ks[5], (DEPTH, DEC_BATCH, WINDOW, N_KV_HEADS, HEAD_DIM), 1.0),
        'state_conv': nrm(ks[6], (DEPTH, DEC_BATCH, CONV_HIST, D_CONV), 0.5),
        'g_mix': gain(ks[7], (DEPTH, D_MODEL)),
        'w_in': nrm(ks[8], (DEPTH, D_MODEL, IN_W), D_MODEL ** -0.5),
        'sinks': nrm(ks[9], (DEPTH, N_Q_HEADS), 0.5),
        'w_dw': nrm(ks[10], (DEPTH, CONV_K, D_CONV), CONV_K ** -0.5),
        'b_dw': nrm(ks[11], (DEPTH, D_CONV), 0.02),
        'ln_g': gain(ks[12], (DEPTH, D_CONV)),
        'ln_b': nrm(ks[13], (DEPTH, D_CONV), 0.02),
        'w_pw2': nrm(ks[14], (DEPTH, D_CONV, D_CONV), D_CONV ** -0.5),
        'g_attn_out': gain(ks[15], (DEPTH, D_ATTN)),
        'g_conv_out': gain(ks[16], (DEPTH, D_CONV)),
        'w_o': nrm(ks[17], (DEPTH, D_MIX, D_MODEL), D_MIX ** -0.5),
        'g_ffn': gain(ks[18], (DEPTH, D_MODEL)),
        'w_up': nrm(ks[19], (DEPTH, D_MODEL, D_FF), D_MODEL ** -0.5),
        'w_down': nrm(ks[20], (DEPTH, D_FF, D_MODEL), D_FF ** -0.5),
        'g_ple': gain(ks[21], (DEPTH, D_MODEL)),
        'w_ple_gate': nrm(ks[22], (DEPTH, D_MODEL, D_MODEL), D_MODEL ** -0.5),
        'w_ple': nrm(ks[23], (DEPTH, D_PLE, D_MODEL), D_PLE ** -0.5),
        'g_final': gain(jax.random.fold_in(key, 99), (D_MODEL,)),
    }


def reference(x_prompt, x_sample, p_prompt, p_sample, cache_k, cache_v, state_conv,
              g_mix, w_in, sinks, w_dw, b_dw, ln_g, ln_b, w_pw2, g_attn_out, g_conv_out, w_o,
              g_ffn, w_up, w_down, g_ple, w_ple_gate, w_ple, g_final):
    yp, ys = x_prompt, x_sample
    bp = x_prompt.shape[0]
    zero_k = jnp.zeros((bp, WINDOW, N_KV_HEADS, HEAD_DIM), x_prompt.dtype)
    zero_u = jnp.zeros((bp, CONV_HIST, D_CONV), x_prompt.dtype)
    kp_l, vp_l, up_l, ks_l, vs_l, us_l = [], [], [], [], [], []
    for i in range(DEPTH):
        wts = (g_mix[i], w_in[i], sinks[i], w_dw[i], b_dw[i], ln_g[i], ln_b[i], w_pw2[i],
               g_attn_out[i], g_conv_out[i], w_o[i], g_ffn[i], w_up[i], w_down[i],
               g_ple[i], w_ple_gate[i], w_ple[i])
        yp, kp, vp, up = layer(yp, p_prompt[i], zero_k, zero_k, zero_u, 0, *wts)
        ys, kk, vv, uu = layer(ys, p_sample[i], cache_k[i], cache_v[i], state_conv[i], PAST_LEN, *wts)
        kp_l.append(kp); vp_l.append(vp); up_l.append(up)
        ks_l.append(kk); vs_l.append(vv); us_l.append(uu)
    y_prompt = rmsnorm(yp, g_final)
    y_sample = rmsnorm(ys, g_final)
    return (y_prompt, y_sample, jnp.stack(kp_l), jnp.stack(vp_l), jnp.stack(up_l),
            jnp.stack(ks_l), jnp.stack(vs_l), jnp.stack(us_l))
```

```python
import numpy as np
from contextlib import ExitStack
import concourse.bass as bass
import concourse.mybir as mybir
from concourse.bass_utils import run_bass_kernel_spmd

F32 = mybir.dt.float32
BF16 = mybir.dt.bfloat16
AF = mybir.ActivationFunctionType
ALU = mybir.AluOpType
AX = mybir.AxisListType

NCORES = 8
DEPTH = 4
D = 1024
SEQ = 2048
NS = 16
TS = 4
NTOK = SEQ + NS * TS
EPS = 1e-6
CONV_K = 31
HIST = 30
NSLOT = 3
SLOT_EL = 8192
TILES = [(0, 512), (512, 512), (1024, 512), (1536, 512), (2048, 64)]
C_GMIX, C_GFFN, C_GPLE, C_GAO, C_GCO, C_LNG, C_LNB, C_BDW, C_WDW = 0, 8, 16, 24, 28, 32, 36, 40, 44
C_PER = 44 + 4 * 31
C_GFIN = DEPTH * C_PER
NCONST = C_GFIN + 8
M_CUR, M_PREV, M_ID, M_SC, M_SN = 0, 128, 256, 384, 640
NMASK = 704


class Sched:
    def __init__(self, nc, es, n_io=24, n_wl=12):
        self.nc = nc
        self.eng = {"pe": nc.tensor, "act": nc.scalar, "dve": nc.vector, "pool": nc.gpsimd, "sp": nc.sync}
        self.sem = {}
        self.cnt = {}
        for name in self.eng:
            self.sem[name] = es.enter_context(nc.semaphore("s_" + name))
            self.cnt[name] = 0
        self.pools = {}
        for pname, n in (("io", n_io), ("wl", n_wl)):
            sems = [es.enter_context(nc.semaphore("s_%s%d" % (pname, i))) for i in range(n)]
            self.pools[pname] = {"sems": sems, "val": [0] * n, "next": 0}
        self.waited = {}
        self.last_w = {}
        self.readers = {}
        self.n_wait = 0
        self.n_ins = 0

    def _semh(self, semkey):
        if isinstance(semkey, str):
            return self.sem[semkey]
        return self.pools[semkey[0]]["sems"][semkey[1]]

    def _wait(self, engname, tok):
        semkey, val, _ = tok
        k = (engname, semkey)
        if self.waited.get(k, 0) >= val:
            return
        self.waited[k] = val
        self.eng[engname].wait_ge(self._semh(semkey), val)
        self.n_wait += 1

    def _deps(self, engname, reads, writes, is_dma):
        for key in reads:
            t = self.last_w.get(key)
            if t is not None:
                if (not is_dma) and t[2] == engname and engname == "pe":
                    continue
                self._wait(engname, t)
            if key == "psT" or (isinstance(key, tuple) and key[0] == "ps"):
                rd = self.readers.get(key)
                if rd:
                    for semkey, (val, src) in rd.items():
                        if src != engname:
                            self._wait(engname, (semkey, val, src))
        for key in writes:
            t = self.last_w.get(key)
            if t is not None:
                if is_dma or t[2] != engname or engname != "pe":
                    self._wait(engname, t)
            rd = self.readers.get(key)
            if rd:
                for semkey, (val, src) in rd.items():
                    if (not is_dma) and src == engname and engname == "pe":
                        continue
                    self._wait(engname, (semkey, val, src))

    def _commit(self, tok, reads, writes):
        for key in writes:
            self.last_w[key] = tok
            self.readers[key] = {}
        for key in reads:
            rd = self.readers.setdefault(key, {})
            old = rd.get(tok[0])
            if old is None or old[0] < tok[1]:
                rd[tok[0]] = (tok[1], tok[2])

    def op(self, engname, fn, reads=(), writes=()):
        self._deps(engname, reads, writes, False)
        ins = fn(self.eng[engname])
        self.cnt[engname] += 1
        ins.then_inc(self.sem[engname], 1)
        tok = (engname, self.cnt[engname], engname)
        self._commit(tok, reads, writes)
        self.n_ins += 1
        return tok

    def dma(self, engname, fns, reads=(), writes=(), pool="io"):
        P = self.pools[pool]
        slot = P["next"]
        P["next"] = (slot + 1) % len(P["sems"])
        if P["val"][slot] > 0:
            self._wait(engname, ((pool, slot), P["val"][slot], None))
        self._deps(engname, reads, writes, True)
        for fn in fns:
            ins = fn(self.eng[engname])
            ins.then_inc(P["sems"][slot], 16)
            P["val"][slot] += 16
            self.n_ins += 1
        tok = ((pool, slot), P["val"][slot], None)
        self._commit(tok, reads, writes)
        return tok

    def barrier(self, engines=("pe", "act", "dve", "pool", "sp")):
        for e in engines:
            for o in ("pe", "act", "dve", "pool"):
                if o != e and self.cnt[o] > 0:
                    self._wait(e, (o, self.cnt[o], o))
            P = self.pools["io"]
            for slot in range(len(P["sems"])):
                if P["val"][slot] > 0:
                    self._wait(e, (("io", slot), P["val"][slot], None))

    def wait_keys(self, engname, keys):
        for key in keys:
            t = self.last_w.get(key)
            if t is not None:
                self._wait(engname, t)


def build_program(n_layers=DEPTH):
    nc = bass.Bass("TRN2", target_bir_lowering=False)

    def din(name, shape):
        return nc.dram_tensor(name, list(shape), F32, kind="ExternalInput").ap()

    def dout(name, shape):
        return nc.dram_tensor(name, list(shape), F32, kind="ExternalOutput").ap()

    x_d = din("xT", [D, NTOK])
    p_d = din("pT", [DEPTH, 256, NTOK])
    kc_d = din("kcT", [DEPTH, 128, NS * 128])
    kcn_d = din("kc_nat", [DEPTH, NS, 128, 128])
    vc_d = din("vc_nat", [DEPTH, NS, 128, 128])
    sc_d = din("scT", [DEPTH, 512, NS, HIST])
    scn_d = din("sc_nat", [DEPTH, NS, HIST, 512])
    win_d = din("w_in", [DEPTH, D, 1792])
    wpw_d = din("w_pw2", [DEPTH, 512, 512])
    wo_d = din("w_o", [DEPTH, D, D])
    wup_d = din("w_up", [DEPTH, D, 4096])
    wdn_d = din("w_down", [DEPTH, 4096, D])
    wg_d = din("w_gate", [DEPTH, D, D])
    wpl_d = din("w_ple", [DEPTH, 256, D])
    const_d = din("consts", [128, NCONST])
    sink_d = din("sinks_b", [128, DEPTH * 8])
    mask_d = din("masks", [128, NMASK])
    cos_d = din("cosT", [128, NTOK])
    sin_d = din("sinT", [128, NTOK])

    y_d = dout("yT", [D, NTOK])
    kwp_d = dout("kwp", [DEPTH, 128, 128])
    vwp_d = dout("vwp", [DEPTH, 128, 128])
    cvp_d = dout("cvp", [DEPTH, 512, HIST])
    kwsa_d = dout("kws_a", [DEPTH, NS, 124, 128])
    kwsb_d = dout("kws_b", [DEPTH, 128, NS * TS])
    vwsa_d = dout("vws_a", [DEPTH, NS, 124, 128])
    vwsb_d = dout("vws_b", [DEPTH, NS * TS, 128])
    cvsa_d = dout("cvs_a", [DEPTH, NS, 26, 512])
    cvsb_d = dout("cvs_b", [DEPTH, 512, NS * TS])
    out_keys = []

    with ExitStack() as es:
        S = Sched(nc, es)

        uniq = [0]

        def sb(stack, name, shape, dt):
            uniq[0] += 1
            return stack.enter_context(nc.sbuf_tensor("%s_%d" % (name, uniq[0]), list(shape), dt))

        xT = sb(es, "xTs", [128, 8, NTOK], F32)
        slots = [sb(es, "slot%d" % i, [128, SLOT_EL], BF16) for i in range(NSLOT)]
        wple = [sb(es, "wple0", [128, 2, D], BF16)] * 2
        consts = sb(es, "constS", [128, NCONST], F32)
        esink = sb(es, "esink", [128, DEPTH * 8], F32)
        masks = sb(es, "maskS", [128, NMASK], BF16)
        ones = sb(es, "ones", [128, 128], BF16)
        epst = sb(es, "epst", [128, 1], F32)
        psb = [es.enter_context(nc.psum_tensor("psb%d" % i, [128, 512], F32)) for i in range(7)]
        psT = es.enter_context(nc.psum_tensor("psT", [128, 1024], BF16))
        ps_rr = [0]
        conv_cnt = [0]

        def next_ps():
            b = ps_rr[0]
            ps_rr[0] = (b + 1) % 6
            return b

        xv = x_d.rearrange("(c p) t -> p c t", p=128)
        def load_x(ti):
            t0, n = TILES[ti]
            S.dma("sp", [lambda e: e.dma_start(out=xT[:, :, t0:t0 + n], in_=xv[:, :, t0:t0 + n])],
                  writes=[("x", ti, c) for c in range(8)])

        S.dma("sp", [lambda e: e.dma_start(out=consts[:, :], in_=const_d)], writes=["consts"])
        load_x(0)
        S.dma("sp", [lambda e: e.dma_start(out=esink[:, :], in_=sink_d)], writes=["esink"])
        S.dma("pool", [lambda e: e.dma_start(out=masks[:, :], in_=mask_d)], writes=["masks"], pool="wl")
        for ti in range(1, len(TILES)):
            load_x(ti)
        S.op("dve", lambda e: e.memset(ones[:, :], 1.0 / 1024.0), writes=["ones"])
        S.op("dve", lambda e: e.memset(epst[:, :], EPS), writes=["epst"])
        S.op("act", lambda e: e.activation(out=esink[:, :], in_=esink[:, :], func=AF.Exp), reads=["esink"], writes=["esink"])

        pieces = []
        for l in range(n_layers):
            winv = win_d[l].rearrange("(c p) n -> p c n", p=128)
            pieces.append((("in_b", l), [
                (lambda s: s[:, :].rearrange("p (c n) -> p c n", c=8), winv[:, :, 768:1792]),
            ]))
            pieces.append((("in_a", l), [
                (lambda s: s[:, 0:6144].rearrange("p (c n) -> p c n", c=8), winv[:, :, 0:768]),
                (lambda s: s[:, 6144:8192].rearrange("p (c n) -> p c n", c=4), wpw_d[l].rearrange("(c p) n -> p c n", p=128)),
            ]))
            pieces.append((("o", l), [
                (lambda s: s[:, :].rearrange("p (c n) -> p c n", c=8), wo_d[l].rearrange("(c p) n -> p c n", p=128)),
            ]))
            for g in range(8):
                pieces.append((("ffn", l, g), [
                    (lambda s: s[:, 0:4096].rearrange("p (c n) -> p c n", c=8),
                     wup_d[l].rearrange("(c p) n -> p c n", p=128)[:, :, g * 512:(g + 1) * 512]),
                    (lambda s: s[:, 4096:8192].rearrange("p (c n) -> p c n", c=4),
                     wdn_d[l, g * 512:(g + 1) * 512, :].rearrange("(c p) n -> p c n", p=128)),
                ]))
            pieces.append((("gate", l), [
                (lambda s: s[:, :].rearrange("p (c n) -> p c n", c=8), wg_d[l].rearrange("(c p) n -> p c n", p=128)),
            ]))
        piece_slot = {}
        wstate = {"next": 0}

        def issue_next_piece(force=None):
            if force is None:
                i = wstate["next"]
                if i >= len(pieces):
                    return
                wstate["next"] = i + 1
            else:
                i = force
            name, parts = pieces[i]
            si = i % NSLOT
            piece_slot[name] = si
            fns = [(lambda e, d=dst, s_=src: e.dma_start(out=d(slots[si]), in_=s_)) for dst, src in parts]
            S.dma("pool", fns, writes=[("slot", si)], pool="wl")

        def load_wple(l):
            S.dma("pool", [lambda e: e.dma_start(out=wple[l % 2][:, :, :], in_=wpl_d[l].rearrange("(c p) n -> p c n", p=128))],
                  writes=[("wple", 0)], pool="wl")

        for i_ in (1, 0, 2):
            issue_next_piece(force=i_)
        wstate["next"] = NSLOT
        load_wple(0)

        def cc(l, base, j=0):
            col = l * C_PER + base + j
            return consts[:, col:col + 1]

        def rmsnorm_tile(l_gcol, ti, sq, rstd, out_fn, out_keys_fn, final=False):
            t0, n = TILES[ti]
            b = next_ps()
            for half in range(2):
                S.op("act", lambda e: e.activation(out=sq[:, :, 0:n], in_=xT[:, half * 4:half * 4 + 4, t0:t0 + n], func=AF.Square),
                     reads=[("x", ti, c) for c in range(half * 4, half * 4 + 4)], writes=["sq"])
                for c in range(4):
                    S.op("pe", lambda e, c=c: e.matmul(psb[b][:, 0:n], lhsT=ones[:, :], rhs=sq[:, c, 0:n],
                                                        start=(half == 0 and c == 0), stop=(half == 1 and c == 3)),
                         reads=["ones", "sq"], writes=[("ps", b)])
            S.op("act", lambda e: e.activation(out=rstd[:, 0:n], in_=psb[b][:, 0:n], func=AF.Ln, bias=epst[:, :], scale=1.0),
                 reads=[("ps", b), "epst"], writes=["rstd"])
            S.op("act", lambda e: e.activation(out=rstd[:, 0:n], in_=rstd[:, 0:n], func=AF.Exp, scale=-0.5), reads=["rstd"], writes=["rstd"])
            for c in range(8):
                S.op("dve", lambda e, c=c: e.scalar_tensor_tensor(out=out_fn(c), in0=xT[:, c, t0:t0 + n],
                                                                   scalar=consts[:, l_gcol + c:l_gcol + c + 1],
                                                                   in1=rstd[:, 0:n], op0=ALU.mult, op1=ALU.mult),
                     reads=[("x", ti, c), "rstd", "consts"], writes=out_keys_fn(c))

        def mm_group(b, n, lhs_fn, rhs_fn, nk, reads):
            for k in range(nk):
                S.op("pe", lambda e, k=k: e.matmul(psb[b][:, 0:n], lhsT=lhs_fn(k), rhs=rhs_fn(k), start=(k == 0), stop=(k == nk - 1)),
                     reads=reads, writes=[("ps", b)])

        for l in range(n_layers):
            with ExitStack() as pa:
                sq = sb(pa, "sq", [128, 4, 512], BF16)
                rstd = sb(pa, "rstd", [128, 512], F32)
                ft = [sb(pa, "ft%d" % i, [128, 512], F32) for i in range(4)]
                bufX = sb(pa, "bufX", [128, 8, 512], BF16)
                bufY = sb(pa, "bufY", [128, 4, 512], BF16)
                cst = sb(pa, "cst", [128, 512], F32)
                snt = sb(pa, "snt", [128, 512], F32)
                vf32 = sb(pa, "vf32", [128, 128], F32)
                Eball = sb(pa, "Eball", [128, 1024], BF16)
                Eb = [Eball[:, 0:512], Eball[:, 512:1024]]
                att_a = sb(pa, "att_a", [128, 8, 64], F32)
                att_n = sb(pa, "att_n", [128, 512], BF16)
                att_s = sb(pa, "att_s", [128, 16], F32)
                cacc = sb(pa, "cacc", [128, 4, 512], F32)
                ubuf = sb(pa, "ubuf", [128, 4 * NS * 34], BF16)
                ukeep = sb(pa, "ukeep", [128, 4, 64], F32)
                dgb = [sb(pa, "dgb%d" % i, [128, CONV_K, 128], BF16) for i in range(2)]
                kT = sb(pa, "kT", [128, 2048], BF16)
                vtok = sb(pa, "vtok", [128, 16, 2, 65], BF16)
                ksn = sb(pa, "ksn", [128, 64], BF16)
                vsn = sb(pa, "vsn", [64, 2, 65], BF16)
                Es = Eball
                Esn = sb(pa, "Esn", [64, 128], BF16)
                hT = sb(pa, "hTb", [128, 8, 512], BF16)
                mixT = bufX
                q_out = bufY
                sT = sb(pa, "sTb", [128, 4, 512], BF16)
                tA = att_a[:, :, :].rearrange("p h d -> p (h d)")
                tB = Eball[:, :].bitcast(F32)
                uP = ubuf[:, 0:4 * (HIST + 512)].rearrange("p (c t) -> p c t", c=4)
                uS = ubuf[:, 0:4 * NS * 34].rearrange("p (c s t) -> p c s t", c=4, s=NS)

                si_a = lambda: slots[piece_slot[("in_a", l)]]
                si_b = lambda: slots[piece_slot[("in_b", l)]]
                si_o = lambda: slots[piece_slot[("o", l)]]
                key_a = lambda: ("slot", piece_slot[("in_a", l)])
                key_b = lambda: ("slot", piece_slot[("in_b", l)])
                key_o = lambda: ("slot", piece_slot[("o", l)])
                wina = lambda: si_a()[:, 0:6144].rearrange("p (c n) -> p c n", c=8)
                wpw = lambda: si_a()[:, 6144:8192].rearrange("p (c n) -> p c n", c=4)
                winb = lambda: si_b()[:, :].rearrange("p (c n) -> p c n", c=8)
                wo = lambda: si_o()[:, :].rearrange("p (c n) -> p c n", c=8)

                S.op("dve", lambda e: e.memset(vtok[:, :, :, 64:65], 1.0), writes=["vtok_ones"])
                S.op("dve", lambda e: e.memset(vsn[:, :, 64:65], 1.0), writes=["vsn_ones"])
                S.op("dve", lambda e: e.memset(uP[:, :, 0:HIST], 0.0), writes=[("u", c) for c in range(4)])

                def emit_A0(ti):
                    t0, n = TILES[ti]
                    S.dma("sp", [lambda e: e.dma_start(out=cst[:, 0:n], in_=cos_d[:, t0:t0 + n]),
                                 lambda e: e.dma_start(out=snt[:, 0:n], in_=sin_d[:, t0:t0 + n])], writes=["cs"])
                    rmsnorm_tile(l * C_PER + C_GMIX, ti, sq, rstd, lambda c: hT[:, c, 0:n], lambda c: [("hx", c)])

                def gen_A(ti):
                    t0, n = TILES[ti]
                    sample = (ti == 4)
                    nblk = n // 128 if not sample else 0
                    if sample:
                        S.dma("pool", [lambda e: e.dma_start(out=kT[:, :], in_=kc_d[l])], writes=["kT", "kT_prev", "kT_cur"], pool="wl")
                        S.dma("pool", [(lambda e, k_=k_: e.dma_start(out=vtok[:, :, k_, 0:64],
                                                              in_=vc_d[l].rearrange("s p (k d) -> p s k d", k=2)[:, :, k_, :])) for k_ in range(2)],
                              reads=["vtok_ones"], writes=["vtok", "vtok_prev"] + [("vtok", b_) for b_ in range(1, 5)], pool="wl")
                        S.dma("pool", [(lambda e, c_=c_: e.dma_start(out=uS[:, c_, :, 0:HIST],
                                                            in_=sc_d[l].rearrange("(c p) s j -> p c s j", p=128)[:, c_, :, :])) for c_ in range(4)],
                              writes=[("u", c) for c in range(4)], pool="wl")
                        k1 = ("o_kwsa", l); k2 = ("o_vwsa", l); k3 = ("o_cvsa", l)
                        S.dma("sp", [lambda e: e.dma_start(out=kwsa_d[l], in_=kcn_d[l, :, 4:128, :])], writes=[k1])
                        S.dma("sp", [lambda e: e.dma_start(out=vwsa_d[l], in_=vc_d[l, :, 4:128, :])], writes=[k2])
                        S.dma("sp", [lambda e: e.dma_start(out=cvsa_d[l], in_=scn_d[l, :, 4:HIST, :])], writes=[k3])
                        out_keys.extend([k1, k2, k3])
                    hkeys = [("hx", c) for c in range(8)]

                    def rope(b, dst_ap, dst_keys):
                        A_, B_ = ft[0], ft[1]
                        S.op("dve", lambda e: e.tensor_tensor(out=A_[:, 0:n], in0=psb[b][:, 0:n], in1=cst[:, 0:n], op=ALU.mult),
                             reads=[("ps", b), "cs"], writes=["ft0"])
                        for (dst, src) in ((0, 32), (32, 0), (64, 96), (96, 64)):
                            S.op("dve", lambda e, dst=dst, src=src: e.tensor_tensor(out=B_[dst:dst + 32, 0:n], in0=psb[b][src:src + 32, 0:n],
                                                                                 in1=snt[src:src + 32, 0:n], op=ALU.mult),
                                 reads=[("ps", b), "cs"], writes=["ft1"])
                        S.op("dve", lambda e: e.tensor_tensor(out=dst_ap, in0=A_[:, 0:n], in1=B_[:, 0:n], op=ALU.add),
                             reads=["ft0", "ft1"], writes=dst_keys)

                    for c in range(4):
                        b = next_ps()
                        mm_group(b, n, lambda k, c=c: wina()[:, k, c * 128:(c + 1) * 128], lambda k: hT[:, k, 0:n], 8, hkeys + [key_a()])
                        rope(b, q_out[:, c, 0:n], [("by", c)])
                        yield
                    b = next_ps()
                    mm_group(b, n, lambda k: wina()[:, k, 512:640], lambda k: hT[:, k, 0:n], 8, hkeys + [key_a()])
                    rope(b, ft[3][:, 0:n], ["ft3"])
                    yield
                    if not sample:
                        S.op("act", lambda e: e.activation(out=kT[:, 128:128 + n], in_=ft[3][:, 0:n], func=AF.Copy),
                             reads=["ft3"], writes=["kT_cur"])
                        if ti == 3:
                            ko = ("o_kwp", l)
                            S.dma("sp", [lambda e: e.dma_start(out=kwp_d[l], in_=ft[3][:, 384:512])], reads=["ft3"], writes=[ko])
                            out_keys.append(ko)
                    else:
                        S.op("act", lambda e: e.activation(out=ksn[:, :], in_=ft[3][:, 0:64], func=AF.Copy),
                             reads=["ft3"], writes=["ksn"])
                        ko = ("o_kwsb", l)
                        S.dma("sp", [lambda e: e.dma_start(out=kwsb_d[l], in_=ft[3][:, 0:64])], reads=["ft3"], writes=[ko])
                        out_keys.append(ko)
                    if not sample:
                        for blk in range(nblk):
                            b = next_ps()
                            mm_group(b, 128, lambda k, blk=blk: hT[:, k, blk * 128:(blk + 1) * 128], lambda k: wina()[:, k, 640:768], 8,
                                     hkeys + [key_a()])
                            S.op("act", lambda e, blk=blk, b=b: e.activation(out=vtok[:, 1 + blk, :, 0:64],
                                                                              in_=psb[b][:, 0:128].rearrange("p (k d) -> p k d", k=2), func=AF.Copy),
                                 reads=[("ps", b), "vtok_ones"], writes=[("vtok", 1 + blk)])
                            if ti == 3 and blk == 3:
                                S.op("dve", lambda e, b=b: e.tensor_copy(out=vf32[:, :], in_=psb[b][:, 0:128]), reads=[("ps", b)], writes=["vf32"])
                                vo = ("o_vwp", l)
                                S.dma("sp", [lambda e: e.dma_start(out=vwp_d[l], in_=vf32[:, :])], reads=["vf32"], writes=[vo])
                                out_keys.append(vo)
                    else:
                        b = next_ps()
                        for k in range(8):
                            S.op("pe", lambda e, k=k: e.matmul(psb[b][:, 0:128], lhsT=hT[:, k, 0:128], rhs=wina()[:, k, 640:768],
                                                                start=(k == 0), stop=(k == 7)),
                                 reads=hkeys + [key_a()], writes=[("ps", b)])
                        S.op("act", lambda e: e.activation(out=vsn[:, :, 0:64], in_=psb[b][0:64, 0:128].rearrange("p (k d) -> p k d", k=2), func=AF.Copy),
                             reads=[("ps", b), "vsn_ones"], writes=["vsn"])
                        S.op("dve", lambda e: e.tensor_copy(out=vf32[0:64, :], in_=psb[b][0:64, 0:128]), reads=[("ps", b)], writes=["vf32"])
                        vo = ("o_vwsb", l)
                        S.dma("sp", [lambda e: e.dma_start(out=vwsb_d[l], in_=vf32[0:64, :])], reads=["vf32"], writes=[vo])
                        out_keys.append(vo)
                    for c in range(4):
                        ba = next_ps()
                        mm_group(ba, n, lambda k, c=c: winb()[:, k, c * 128:(c + 1) * 128], lambda k: hT[:, k, 0:n], 8, hkeys + [key_b()])
                        bg = next_ps()
                        mm_group(bg, n, lambda k, c=c: winb()[:, k, 512 + c * 128:512 + (c + 1) * 128], lambda k: hT[:, k, 0:n], 8, hkeys + [key_b()])
                        S.op("act", lambda e: e.activation(out=ft[2][:, 0:n], in_=psb[bg][:, 0:n], func=AF.Sigmoid),
                             reads=[("ps", bg)], writes=["ft2"])
                        if not sample:
                            S.op("dve", lambda e, c=c: e.tensor_tensor(out=uP[:, c, HIST:HIST + n], in0=psb[ba][:, 0:n], in1=ft[2][:, 0:n], op=ALU.mult),
                                 reads=[("ps", ba), "ft2"], writes=[("u", c)])
                            if ti == 3:
                                S.op("dve", lambda e, c=c: e.tensor_tensor(out=ukeep[:, c, 0:HIST], in0=psb[ba][:, 482:512], in1=ft[2][:, 482:512], op=ALU.mult),
                                     reads=[("ps", ba), "ft2"], writes=[("ukeep", c)])
                        else:
                            S.op("dve", lambda e, c=c: e.tensor_tensor(out=ukeep[:, c, 0:64], in0=psb[ba][:, 0:64], in1=ft[2][:, 0:64], op=ALU.mult),
                                 reads=[("ps", ba), "ft2"], writes=[("ukeep", c)])
                            S.op("dve", lambda e, c=c: e.tensor_tensor(out=uS[:, c, :, HIST:HIST + TS],
                                                                        in0=psb[ba][:, 0:n].rearrange("p (s t) -> p s t", s=NS),
                                                                        in1=ft[2][:, 0:n].rearrange("p (s t) -> p s t", s=NS), op=ALU.mult),
                                 reads=[("ps", ba), "ft2"], writes=[("u", c)])
                    if sample:
                        issue_next_piece()
                    if ti == 3:
                        co = ("o_cvp", l)
                        S.dma("sp", [lambda e: e.dma_start(out=cvp_d[l].rearrange("(c p) j -> p c j", p=128), in_=ukeep[:, :, 0:HIST])],
                              reads=[("ukeep", c) for c in range(4)], writes=[co])
                        out_keys.append(co)
                    if sample:
                        co = ("o_cvsb", l)
                        S.dma("sp", [lambda e: e.dma_start(out=cvsb_d[l].rearrange("(c p) t -> p c t", p=128), in_=ukeep[:, :, 0:64])],
                              reads=[("ukeep", c) for c in range(4)], writes=[co])
                        out_keys.append(co)

                    yield

                def do_B(ti):
                    t0, n = TILES[ti]
                    sample = (ti == 4)
                    nblk = n // 128 if not sample else 0
                    CB = 6

                    def conv_build(c):
                        gi = conv_cnt[0]
                        conv_cnt[0] += 1
                        par = gi % 2
                        base = l * C_PER + C_WDW + c * 31
                        S.op("pool", lambda e: e.tensor_tensor(
                            out=dgb[par][:, :, :], in0=masks[:, M_ID:M_ID + 128].unsqueeze(1).broadcast_to([128, CONV_K, 128]),
                            in1=consts[:, base:base + CONV_K].unsqueeze(2).broadcast_to([128, CONV_K, 128]), op=ALU.mult),
                             reads=["masks", "consts"], writes=[("dg", par)])
                        return par

                    def conv_taps(c, par, j0, j1):
                        for j in range(j0, j1):
                            if not sample:
                                S.op("pe", lambda e, j=j: e.matmul(psb[CB][:, 0:n], lhsT=dgb[par][:, j, :], rhs=uP[:, c, j:j + n],
                                                                   start=(j == 0), stop=(j == CONV_K - 1)),
                                     reads=[("dg", par), ("u", c)], writes=[("ps", CB)])
                            else:
                                S.op("pe", lambda e, j=j: e.matmul(psb[CB][:, 0:n].rearrange("p (s t) -> p s t", s=NS), lhsT=dgb[par][:, j, :],
                                                                   rhs=uS[:, c, :, j:j + TS], start=(j == 0), stop=(j == CONV_K - 1)),
                                     reads=[("dg", par), ("u", c)], writes=[("ps", CB)])
                        if j1 == CONV_K:
                            S.op("act", lambda e: e.activation(out=cacc[:, c, 0:n], in_=psb[CB][:, 0:n], func=AF.Identity, bias=cc(l, C_BDW, c), scale=1.0),
                                 reads=[("ps", CB), "consts"], writes=[("cacc", c)])

                    cpar = [conv_build(c_) for c_ in range(2)] + [None, None]

                    es8 = esink[:, l * 8:(l + 1) * 8]
                    if not sample:
                        for blk in range(nblk):
                            has_prev = not (ti == 0 and blk == 0)
                            bo = []
                            for kvh in range(2):
                                pr = slice(kvh * 64, (kvh + 1) * 64)
                                qrhs = q_out[pr, :, blk * 128:(blk + 1) * 128]
                                parts = ([("prev", blk * 128, M_PREV, blk)] if has_prev else []) + [("cur", 128 + blk * 128, M_CUR, blk + 1)]
                                ebufs = []
                                for pi, (nm, kcol, mcol, vblk) in enumerate(parts):
                                    b = next_ps()
                                    kkey = "kT_cur" if nm == "cur" or blk > 0 else "kT_prev"
                                    S.op("pe", lambda e, b=b, kcol=kcol: e.matmul(psb[b][:, :].rearrange("p (g q) -> p g q", g=4),
                                                                                   lhsT=kT[pr, kcol:kcol + 128], rhs=qrhs, start=True, stop=True),
                                         reads=[kkey] + [("by", c) for c in range(4)], writes=[("ps", b)])
                                    Ebuf = Eb[pi]
                                    S.op("act", lambda e, b=b, Ebuf=Ebuf: e.activation(out=Ebuf[:, :], in_=psb[b][:, :], func=AF.Exp, scale=0.125),
                                         reads=[("ps", b)], writes=[("Eb", pi)])
                                    S.op("dve", lambda e, Ebuf=Ebuf, mcol=mcol: e.tensor_tensor(
                                        out=Ebuf[:, :].rearrange("p (g q) -> p g q", g=4), in0=Ebuf[:, :].rearrange("p (g q) -> p g q", g=4),
                                        in1=masks[:, mcol:mcol + 128].unsqueeze(1).broadcast_to([128, 4, 128]), op=ALU.mult),
                                         reads=[("Eb", pi), "masks"], writes=[("Eb", pi)])
                                    ebufs.append((pi, vblk))
                                conv_taps(blk, cpar[blk], kvh * 8, kvh * 8 + 8)
                                b = next_ps()
                                bo.append(b)
                                for g in range(4):
                                    for ii, (pi, vblk) in enumerate(ebufs):
                                        vkey = ("vtok", vblk) if vblk > 0 else "vtok_prev"
                                        S.op("pe", lambda e, b=b, g=g, pi=pi, vblk=vblk, ii=ii: e.matmul(
                                            psb[b][:, g * 65:(g + 1) * 65], lhsT=Eb[pi][:, g * 128:(g + 1) * 128], rhs=vtok[:, vblk, kvh, :],
                                            start=(ii == 0), stop=(ii == len(ebufs) - 1)),
                                             reads=[("Eb", pi), vkey, "vtok_ones"], writes=[("ps", b)])
                            conv_taps(blk, cpar[blk], 16, CONV_K)
                            if blk + 2 < 4:
                                cpar[blk + 2] = conv_build(blk + 2)
                            for kvh in range(2):
                                b = bo[kvh]
                                pv = psb[b][:, 0:260].rearrange("p (g d) -> p g d", g=4)
                                S.op("dve", lambda e, pv=pv, kvh=kvh: e.tensor_tensor(out=att_s[:, kvh * 4:(kvh + 1) * 4], in0=pv[:, :, 64],
                                                                                      in1=es8[:, kvh * 4:(kvh + 1) * 4], op=ALU.add),
                                     reads=[("ps", b), "esink"], writes=[("att_s", kvh)])
                                S.op("dve", lambda e, kvh=kvh: e.reciprocal(out=att_s[:, kvh * 4:(kvh + 1) * 4], in_=att_s[:, kvh * 4:(kvh + 1) * 4]),
                                     reads=[("att_s", kvh)], writes=[("att_s", kvh)])
                                S.op("dve", lambda e, pv=pv, kvh=kvh: e.tensor_tensor(
                                    out=att_a[:, kvh * 4:(kvh + 1) * 4, :], in0=pv[:, :, 0:64],
                                    in1=att_s[:, kvh * 4:(kvh + 1) * 4].unsqueeze(2).broadcast_to([128, 4, 64]), op=ALU.mult),
                                     reads=[("ps", b), ("att_s", kvh)], writes=[("att_a", kvh)])
                            attn_finish(S, l, n_rows=128, att_a=att_a, att_n=att_n, att_s=att_s, ft=ft, epst=epst, masks=masks, psT=psT,
                                        consts=consts, mix_dst=lambda c, blk=blk: mixT[:, c, blk * 128:(blk + 1) * 128], par=blk % 2)
                            if blk == 1 and ti + 1 < len(TILES):
                                emit_A0(ti + 1)
                        if ti < 3:
                            S.op("act", lambda e: e.activation(out=kT[:, 0:128], in_=kT[:, 512:640], func=AF.Copy),
                                 reads=["kT_cur"], writes=["kT_prev"])
                            S.op("act", lambda e: e.activation(out=vtok[:, 0, :, 0:64], in_=vtok[:, 4, :, 0:64], func=AF.Copy),
                                 reads=[("vtok", 4)], writes=["vtok_prev"])
                    else:
                        for c in range(4):
                            conv_taps(c, cpar[c], 0, CONV_K)
                            if c + 2 < 4:
                                cpar[c + 2] = conv_build(c + 2)
                        for hf in range(2):
                            bo = []
                            for kvh in range(2):
                                pr = slice(kvh * 64, (kvh + 1) * 64)
                                qrhs = q_out[pr, :, hf * 32:(hf + 1) * 32]
                                bcs = []
                                for half2 in range(2):
                                    b = next_ps()
                                    bcs.append(b)
                                    for sq_ in range(4):
                                        s_ = hf * 8 + half2 * 4 + sq_
                                        S.op("pe", lambda e, b=b, sq_=sq_, s_=s_: e.matmul(
                                            psb[b][:, sq_ * 128:(sq_ + 1) * 128].rearrange("p (g q) -> p g q", g=4),
                                            lhsT=kT[pr, s_ * 128:(s_ + 1) * 128], rhs=qrhs, start=True, stop=True),
                                             reads=["kT"] + [("by", c) for c in range(4)], writes=[("ps", b)])
                                    S.op("act", lambda e, b=b, half2=half2: e.activation(out=Es[:, half2 * 512:(half2 + 1) * 512], in_=psb[b][:, :],
                                                                                       func=AF.Exp, scale=0.125),
                                         reads=[("ps", b)], writes=[("Eb", half2)])
                                    S.op("dve", lambda e, half2=half2: e.tensor_tensor(
                                        out=Es[:, half2 * 512:(half2 + 1) * 512].rearrange("p (a g q) -> p a g q", a=4, g=4),
                                        in0=Es[:, half2 * 512:(half2 + 1) * 512].rearrange("p (a g q) -> p a g q", a=4, g=4),
                                        in1=masks[:, M_SC + half2 * 128:M_SC + (half2 + 1) * 128].rearrange("p (a q) -> p a q", a=4)
                                        .unsqueeze(2).broadcast_to([128, 4, 4, 32]), op=ALU.mult),
                                         reads=[("Eb", half2), "masks"], writes=[("Eb", half2)])
                                b = next_ps()
                                S.op("pe", lambda e, b=b: e.matmul(psb[b][0:64, 0:128].rearrange("p (g q) -> p g q", g=4),
                                                                    lhsT=ksn[pr, :], rhs=qrhs, start=True, stop=True),
                                     reads=["ksn"] + [("by", c) for c in range(4)], writes=[("ps", b)])
                                S.op("act", lambda e, b=b: e.activation(out=Esn[:, :], in_=psb[b][0:64, 0:128], func=AF.Exp, scale=0.125),
                                     reads=[("ps", b)], writes=["Esn"])
                                S.op("dve", lambda e: e.tensor_tensor(
                                    out=Esn[:, :].rearrange("p (g q) -> p g q", g=4), in0=Esn[:, :].rearrange("p (g q) -> p g q", g=4),
                                    in1=masks[0:64, M_SN + hf * 32:M_SN + (hf + 1) * 32].unsqueeze(1).broadcast_to([64, 4, 32]), op=ALU.mult),
                                     reads=["Esn", "masks"], writes=["Esn"])
                                b = next_ps()
                                bo.append(b)
                                for sq_ in range(8):
                                    s_ = hf * 8 + sq_
                                    S.op("pe", lambda e, b=b, sq_=sq_, s_=s_: e.matmul(
                                        psb[b][:, 0:65], lhsT=Es[:, sq_ * 128:(sq_ + 1) * 128], rhs=vtok[:, s_, kvh, :],
                                        start=(sq_ == 0), stop=False),
                                         reads=[("Eb", sq_ // 4), "vtok", "vtok_ones"], writes=[("ps", b)])
                                S.op("pe", lambda e, b=b: e.matmul(psb[b][:, 0:65], lhsT=Esn[:, :], rhs=vsn[:, kvh, :], start=False, stop=True),
                                     reads=["Esn", "vsn", "vsn_ones"], writes=[("ps", b)])
                            for kvh in range(2):
                                b = bo[kvh]
                                for g in range(4):
                                    h = kvh * 4 + g
                                    rows_src = slice(g * 32, (g + 1) * 32)
                                    rows_dst = slice(hf * 32, (hf + 1) * 32)
                                    S.op("dve", lambda e, b=b, h=h, rows_src=rows_src: e.tensor_tensor(
                                        out=ft[3][rows_src, h:h + 1], in0=psb[b][rows_src, 64:65], in1=es8[rows_src, h:h + 1], op=ALU.add),
                                         reads=[("ps", b), "esink"], writes=["ft3"])
                                    S.op("dve", lambda e, h=h, rows_src=rows_src: e.reciprocal(out=ft[3][rows_src, h:h + 1], in_=ft[3][rows_src, h:h + 1]),
                                         reads=["ft3"], writes=["ft3"])
                                    S.op("dve", lambda e, b=b, h=h, rows_src=rows_src: e.tensor_scalar(
                                        out=ft[0][rows_src, h * 64:(h + 1) * 64], in0=psb[b][rows_src, 0:64], scalar1=ft[3][rows_src, h:h + 1],
                                        scalar2=None, op0=ALU.mult),
                                         reads=[("ps", b), "ft3"], writes=["ft0"])
                                    S.op("dve", lambda e, h=h, rows_src=rows_src, rows_dst=rows_dst: e.tensor_copy(
                                        out=att_a[rows_dst, h, :], in_=ft[0][rows_src, h * 64:(h + 1) * 64]),
                                         reads=["ft0"], writes=[("att_a", kvh)])
                        attn_finish(S, l, n_rows=64, att_a=att_a, att_n=att_n, att_s=att_s, ft=ft, epst=epst, masks=masks, psT=psT,
                                    consts=consts, mix_dst=lambda c: mixT[:, c, 0:64], par=0)

                    if ti < 3:
                        S.op("act", lambda e: e.activation(out=uP[:, :, 0:HIST], in_=uP[:, :, 512:542], func=AF.Copy),
                             reads=[("u", c) for c in range(4)], writes=[("u", c) for c in range(4)])
                    if ti == 3:
                        pass

                def gen_C(ti):
                    t0, n = TILES[ti]
                    sample = (ti == 4)
                    nblk = n // 128 if not sample else 0
                    ckeys = [("cacc", c) for c in range(4)]
                    S.op("act", lambda e: e.activation(out=sq[:, :, 0:n], in_=cacc[:, :, 0:n], func=AF.Copy), reads=ckeys, writes=["sq"])
                    b1 = next_ps()
                    mm_group(b1, n, lambda k: ones[:, :], lambda k: sq[:, k, 0:n], 4, ["ones", "sq"])
                    S.op("act", lambda e: e.activation(out=tA[:, 0:n], in_=psb[b1][:, 0:n], func=AF.Identity, scale=2.0),
                         reads=[("ps", b1)], writes=["c_mean"])
                    S.op("act", lambda e: e.activation(out=sq[:, :, 0:n], in_=cacc[:, :, 0:n], func=AF.Square), reads=ckeys, writes=["sq"])
                    b2 = next_ps()
                    mm_group(b2, n, lambda k: ones[:, :], lambda k: sq[:, k, 0:n], 4, ["ones", "sq"])
                    S.op("dve", lambda e: e.tensor_tensor(out=tB[:, 0:n], in0=tA[:, 0:n], in1=tA[:, 0:n], op=ALU.mult),
                         reads=["c_mean"], writes=["c_rstd"])
                    S.op("dve", lambda e: e.scalar_tensor_tensor(out=tB[:, 0:n], in0=psb[b2][:, 0:n], scalar=2.0, in1=tB[:, 0:n],
                                                                 op0=ALU.mult, op1=ALU.subtract),
                         reads=[("ps", b2), "c_rstd"], writes=["c_rstd"])
                    S.op("dve", lambda e: e.tensor_scalar_max(out=tB[:, 0:n], in0=tB[:, 0:n], scalar1=0.0), reads=["c_rstd"], writes=["c_rstd"])
                    S.op("act", lambda e: e.activation(out=tB[:, 0:n], in_=tB[:, 0:n], func=AF.Ln, bias=epst[:, :], scale=1.0),
                         reads=["c_rstd", "epst"], writes=["c_rstd"])
                    S.op("act", lambda e: e.activation(out=tB[:, 0:n], in_=tB[:, 0:n], func=AF.Exp, scale=-0.5), reads=["c_rstd"], writes=["c_rstd"])
                    yield
                    for c in range(4):
                        S.op("dve", lambda e, c=c: e.tensor_tensor(out=ft[2][:, 0:n], in0=cacc[:, c, 0:n], in1=tA[:, 0:n], op=ALU.subtract),
                             reads=[("cacc", c), "c_mean"], writes=["ft2"])
                        S.op("dve", lambda e: e.tensor_tensor(out=ft[2][:, 0:n], in0=ft[2][:, 0:n], in1=tB[:, 0:n], op=ALU.mult),
                             reads=["ft2", "c_rstd"], writes=["ft2"])
                        S.op("act", lambda e, c=c: e.activation(out=sT[:, c, 0:n], in_=ft[2][:, 0:n], func=AF.Silu,
                                                                 bias=cc(l, C_LNB, c), scale=cc(l, C_LNG, c)),
                             reads=["ft2", "consts"], writes=[("sy", c)])
                        yield
                    for oc in range(4):
                        b = next_ps()
                        mm_group(b, n, lambda k, oc=oc: wpw()[:, k, oc * 128:(oc + 1) * 128], lambda k: sT[:, k, 0:n], 4,
                                 [("sy", c) for c in range(4)] + [key_a()])
                        S.op("act", lambda e, oc=oc, b=b: e.activation(out=cacc[:, oc, 0:n], in_=psb[b][:, 0:n], func=AF.Copy),
                             reads=[("ps", b)], writes=[("cacc", oc)])
                        S.op("act", lambda e, oc=oc, b=b: e.activation(out=sq[:, oc, 0:n], in_=psb[b][:, 0:n], func=AF.Square),
                             reads=[("ps", b)], writes=["sq"])
                        yield
                    b3 = next_ps()
                    mm_group(b3, n, lambda k: ones[:, :], lambda k: sq[:, k, 0:n], 4, ["ones", "sq"])
                    S.op("act", lambda e: e.activation(out=rstd[:, 0:n], in_=psb[b3][:, 0:n], func=AF.Ln, bias=epst[:, :], scale=2.0),
                         reads=[("ps", b3), "epst"], writes=["rstd"])
                    S.op("act", lambda e: e.activation(out=rstd[:, 0:n], in_=rstd[:, 0:n], func=AF.Exp, scale=-0.5), reads=["rstd"], writes=["rstd"])
                    for oc in range(4):
                        S.op("dve", lambda e, oc=oc: e.scalar_tensor_tensor(out=mixT[:, 4 + oc, 0:n], in0=cacc[:, oc, 0:n], scalar=cc(l, C_GCO, oc),
                                                                            in1=rstd[:, 0:n], op0=ALU.mult, op1=ALU.mult),
                             reads=[("cacc", oc), "rstd", "consts"], writes=[("bx", 4 + oc)])
                    for oc in range(8):
                        b = next_ps()
                        mm_group(b, n, lambda k, oc=oc: wo()[:, k, oc * 128:(oc + 1) * 128], lambda k: mixT[:, k, 0:n], 8,
                                 [("bx", c) for c in range(8)] + [key_o()])
                        S.op("dve", lambda e, oc=oc, b=b: e.tensor_tensor(out=xT[:, oc, t0:t0 + n], in0=psb[b][:, 0:n], in1=xT[:, oc, t0:t0 + n], op=ALU.add),
                             reads=[("ps", b), ("x", ti, oc)], writes=[("x", ti, oc)])
                        yield
                    yield

                def drain(g):
                    for _ in g:
                        pass

                def zipgens(ga, gb, ratio=1):
                    da = db = False
                    while not (da and db):
                        if not da:
                            try:
                                next(ga)
                            except StopIteration:
                                da = True
                        if not db:
                            try:
                                next(gb)
                            except StopIteration:
                                db = True

                emit_A0(0)
                drain(gen_A(0))
                for ti in range(len(TILES)):
                    do_B(ti)
                    if ti + 1 < len(TILES):
                        zipgens(gen_A(ti + 1), gen_C(ti))
                    else:
                        drain(gen_C(ti))
                issue_next_piece()
                issue_next_piece()
                S.barrier()
            with ExitStack() as pe_:
                h2 = sb(pe_, "h2", [128, 8, NTOK], BF16)
                sq = sb(pe_, "sqE", [128, 4, 512], BF16)
                rstd = sb(pe_, "rstdE", [128, 512], F32)
                hid = [sb(pe_, "hid%d" % i, [128, 4, 512], BF16) for i in range(2)]
                rl = [sb(pe_, "rl%d" % i, [128, 512], F32) for i in range(2)]
                def h2_norm(tj):
                    tj0, nj = TILES[tj]
                    rmsnorm_tile(l * C_PER + C_GFFN, tj, sq, rstd, lambda c: h2[:, c, tj0:tj0 + nj], lambda c: [("h2", tj, c)])

                items = [(g, ti) for g in range(8) for ti in range(len(TILES))]

                def ffn_w(g):
                    sl = slots[piece_slot[("ffn", l, g)]]
                    return (sl[:, 0:4096].rearrange("p (c n) -> p c n", c=8), sl[:, 4096:8192].rearrange("p (c n) -> p c n", c=4),
                            ("slot", piece_slot[("ffn", l, g)]))

                def ffn_up(idx):
                    g, ti = items[idx]
                    t0, n = TILES[ti]
                    wup, _, skey = ffn_w(g)
                    hb = hid[idx % 2]
                    for hc in range(4):
                        b = next_ps()
                        mm_group(b, n, lambda k, hc=hc: wup[:, k, hc * 128:(hc + 1) * 128], lambda k: h2[:, k, t0:t0 + n], 8,
                                 [("h2", ti, c) for c in range(8)] + [skey])
                        r = rl[hc % 2]
                        S.op("act", lambda e, b=b, r=r: e.activation(out=r[:, 0:n], in_=psb[b][:, 0:n], func=AF.Relu),
                             reads=[("ps", b)], writes=[("rl", hc % 2)])
                        S.op("pool", lambda e, r=r, hc=hc: e.tensor_tensor(out=hb[:, hc, 0:n], in0=r[:, 0:n], in1=r[:, 0:n], op=ALU.mult),
                             reads=[("rl", hc % 2)], writes=[("hid", idx % 2, hc)])

                def ffn_down(idx):
                    g, ti = items[idx]
                    t0, n = TILES[ti]
                    _, wdn, skey = ffn_w(g)
                    hb = hid[idx % 2]
                    for oc in range(8):
                        b = next_ps()
                        mm_group(b, n, lambda k, oc=oc: wdn[:, k, oc * 128:(oc + 1) * 128], lambda k: hb[:, k, 0:n], 4,
                                 [("hid", idx % 2, c) for c in range(4)] + [skey])
                        S.op("dve", lambda e, oc=oc, b=b: e.tensor_tensor(out=xT[:, oc, t0:t0 + n], in0=psb[b][:, 0:n], in1=xT[:, oc, t0:t0 + n], op=ALU.add),
                             reads=[("ps", b), ("x", ti, oc)], writes=[("x", ti, oc)])

                h2_norm(0)
                ffn_up(0)
                h2_norm(1)
                for idx in range(len(items)):
                    g, ti = items[idx]
                    if idx + 1 < len(items):
                        g1, t1 = items[idx + 1]
                        if g1 == 0 and t1 + 1 < len(TILES):
                            h2_norm(t1 + 1)
                        ffn_up(idx + 1)
                    ffn_down(idx)
                    if ti == len(TILES) - 1:
                        issue_next_piece()
                S.barrier()
            with ExitStack() as pf:
                sq = sb(pf, "sqF", [128, 4, 512], BF16)
                rstd = sb(pf, "rstdF", [128, 512], F32)
                hTF = [sb(pf, "hTF%d" % i, [128, 8, 512], BF16) for i in range(2)]
                pT = [sb(pf, "pTF%d" % i, [128, 2, 512], BF16) for i in range(2)]
                sg = [sb(pf, "sgF%d" % i, [128, 512], F32) for i in range(2)]
                tmpf = [sb(pf, "tmpF%d" % i, [128, 512], F32) for i in range(2)]
                yt = sb(pf, "ytF", [128, 8, 512], F32)
                sl = lambda: slots[piece_slot[("gate", l)]]
                skey = lambda: ("slot", piece_slot[("gate", l)])
                wg = lambda: sl()[:, :].rearrange("p (c n) -> p c n", c=8)
                wp = wple[l % 2]
                def f_pre(tj):
                    tj0, nj = TILES[tj]
                    pbj = pT[tj % 2]
                    hj = hTF[tj % 2]
                    S.dma("pool", [lambda e: e.dma_start(out=pbj[:, :, 0:nj], in_=p_d[l].rearrange("(c p) t -> p c t", p=128)[:, :, tj0:tj0 + nj])],
                          writes=[("pT", tj % 2)], pool="wl")
                    rmsnorm_tile(l * C_PER + C_GPLE, tj, sq, rstd, lambda c: hj[:, c, 0:nj], lambda c: [("hTF", tj % 2, c)])

                f_pre(0)
                for ti, (t0, n) in enumerate(TILES):
                    pb = pT[ti % 2]
                    hT = hTF[ti % 2]
                    if ti + 1 < len(TILES):
                        f_pre(ti + 1)
                    for oc in range(8):
                        bg = next_ps()
                        mm_group(bg, n, lambda k, oc=oc: wg()[:, k, oc * 128:(oc + 1) * 128], lambda k: hT[:, k, 0:n], 8,
                                 [("hTF", ti % 2, c) for c in range(8)] + [skey()])
                        bp = next_ps()
                        mm_group(bp, n, lambda k, oc=oc: wp[:, k, oc * 128:(oc + 1) * 128], lambda k: pb[:, k, 0:n], 2,
                                 [("pT", ti % 2), ("wple", 0)])
                        s_ = sg[oc % 2]
                        t_ = tmpf[oc % 2]
                        S.op("act", lambda e, s_=s_, bg=bg: e.activation(out=s_[:, 0:n], in_=psb[bg][:, 0:n], func=AF.Sigmoid),
                             reads=[("ps", bg)], writes=[("sgF", oc % 2)])
                        S.op("dve", lambda e, s_=s_, t_=t_, bp=bp: e.tensor_tensor(out=t_[:, 0:n], in0=psb[bp][:, 0:n], in1=s_[:, 0:n], op=ALU.mult),
                             reads=[("ps", bp), ("sgF", oc % 2)], writes=[("tmpF", oc % 2)])
                        S.op("pool", lambda e, t_=t_, oc=oc: e.tensor_tensor(out=xT[:, oc, t0:t0 + n], in0=t_[:, 0:n], in1=xT[:, oc, t0:t0 + n], op=ALU.add),
                             reads=[("tmpF", oc % 2), ("x", ti, oc)], writes=[("x", ti, oc)])
                    if l == n_layers - 1:
                        rmsnorm_tile(C_GFIN, ti, sq, rstd, lambda c: yt[:, c, 0:n], lambda c: [("yt", c)])
                        yo = ("o_y", ti)
                        S.dma("sp", [lambda e: e.dma_start(out=y_d.rearrange("(c p) t -> p c t", p=128)[:, :, t0:t0 + n], in_=yt[:, :, 0:n])],
                              reads=[("yt", c) for c in range(8)], writes=[yo])
                        out_keys.append(yo)
                issue_next_piece()
                if l + 1 < n_layers:
                    load_wple(l + 1)
                S.barrier()
        S.wait_keys("sp", out_keys)
        S.barrier(engines=("sp",))
    return nc


def attn_finish(S, l, n_rows, att_a, att_n, att_s, ft, epst, masks, psT, consts, mix_dst, par):
    R = slice(0, n_rows)
    a2 = att_a[R, :, :].rearrange("p h d -> p (h d)")
    S.op("act", lambda e: e.activation(out=ft[2][R, :], in_=a2, func=AF.Square), reads=[("att_a", 0), ("att_a", 1)], writes=["ft2"])
    S.op("dve", lambda e: e.reduce_sum(out=att_s[R, 8:9], in_=ft[2][R, :], axis=AX.X), reads=["ft2"], writes=["att_ss"])
    S.op("act", lambda e: e.activation(out=att_s[R, 8:9], in_=att_s[R, 8:9], func=AF.Ln, bias=epst[R, :], scale=1.0 / 512.0),
         reads=["att_ss", "epst"], writes=["att_ss"])
    S.op("act", lambda e: e.activation(out=att_s[R, 8:9], in_=att_s[R, 8:9], func=AF.Exp, scale=-0.5), reads=["att_ss"], writes=["att_ss"])
    S.op("act", lambda e: e.activation(out=att_n[R, :], in_=a2, func=AF.Identity, scale=att_s[R, 8:9]),
         reads=[("att_a", 0), ("att_a", 1), "att_ss"], writes=["att_n"])
    for c in range(4):
        pt = psT[:, par * 512 + c * 128: par * 512 + c * 128 + n_rows]
        S.op("pe", lambda e, c=c, pt=pt: e.transpose(pt, att_n[R, c * 128:(c + 1) * 128], masks[R, M_ID:M_ID + n_rows]),
             reads=["att_n", "masks"], writes=["psT"])
    for c in range(4):
        pt = psT[:, par * 512 + c * 128: par * 512 + c * 128 + n_rows]
        S.op("act", lambda e, c=c, pt=pt: e.activation(out=mix_dst(c), in_=pt, func=AF.Identity,
                                                        scale=consts[:, l * C_PER + C_GAO + c:l * C_PER + C_GAO + c + 1]),
             reads=["psT", "consts"], writes=[("bx", c)])


def _host_constants():
    half = 32
    inv = 1.0 / (10000.0 ** (np.arange(half, dtype=np.float64) / half))
    pos = np.concatenate([np.arange(SEQ, dtype=np.float64), np.tile(8192.0 + np.arange(TS, dtype=np.float64), NS)])
    ang = pos[None, :] * inv[:, None]
    cos32 = np.cos(ang).astype(np.float32)
    sin32 = np.sin(ang).astype(np.float32)
    lane = np.arange(128)
    f = lane % 32
    sign = np.where((lane % 64) < 32, 1.0, -1.0).astype(np.float32)
    cosT = cos32[f, :]
    sinT = sin32[f, :] * sign[:, None]
    masks = np.zeros((128, NMASK), np.float32)
    li = np.arange(128)[:, None]
    qi = np.arange(128)[None, :]
    masks[:, M_CUR:M_CUR + 128] = (li <= qi)
    masks[:, M_PREV:M_PREV + 128] = (li >= qi)
    masks[:, M_ID:M_ID + 128] = np.eye(128)
    msc = np.zeros((128, 8, 8, 4), np.float32)
    for sq in range(8):
        for t in range(4):
            msc[:, sq, sq, t] = (np.arange(128) >= t)
    masks[:, M_SC:M_SC + 256] = msc.reshape(128, 256)
    msn = np.zeros((64, 2, 8, 4), np.float32)
    for sk in range(16):
        for tp in range(4):
            for t in range(4):
                if tp <= t:
                    msn[sk * 4 + tp, sk // 8, sk % 8, t] = 1.0
    masks[0:64, M_SN:M_SN + 64] = msn.reshape(64, 64)
    return np.ascontiguousarray(cosT), np.ascontiguousarray(sinT), masks


_CACHE = {}
N_LAYERS = DEPTH


def kernel(x_prompt, x_sample, p_prompt, p_sample, cache_k, cache_v, state_conv,
           g_mix, w_in, sinks, w_dw, b_dw, ln_g, ln_b, w_pw2, g_attn_out, g_conv_out, w_o,
           g_ffn, w_up, w_down, g_ple, w_ple_gate, w_ple, g_final):
    f = lambda a: np.ascontiguousarray(np.asarray(a, dtype=np.float32))
    x_prompt, x_sample, p_prompt, p_sample = f(x_prompt), f(x_sample), f(p_prompt), f(p_sample)
    cache_k, cache_v, state_conv = f(cache_k), f(cache_v), f(state_conv)
    w_in = f(w_in)
    qperm = np.concatenate([np.concatenate([np.arange(c * 64, (c + 1) * 64), np.arange((4 + c) * 64, (5 + c) * 64)]) for c in range(4)])
    colperm = np.concatenate([qperm, np.arange(512, 1792)])
    w_in_p = np.ascontiguousarray(w_in[:, :, colperm])
    consts = np.zeros((128, NCONST), np.float32)

    def fm(v, nchunk):
        return np.asarray(v, np.float32).reshape(nchunk, 128).T

    for l in range(DEPTH):
        o = l * C_PER
        consts[:, o + C_GMIX:o + C_GMIX + 8] = fm(g_mix[l], 8)
        consts[:, o + C_GFFN:o + C_GFFN + 8] = fm(g_ffn[l], 8)
        consts[:, o + C_GPLE:o + C_GPLE + 8] = fm(g_ple[l], 8)
        consts[:, o + C_GAO:o + C_GAO + 4] = fm(g_attn_out[l], 4)
        consts[:, o + C_GCO:o + C_GCO + 4] = fm(g_conv_out[l], 4)
        consts[:, o + C_LNG:o + C_LNG + 4] = fm(ln_g[l], 4)
        consts[:, o + C_LNB:o + C_LNB + 4] = fm(ln_b[l], 4)
        consts[:, o + C_BDW:o + C_BDW + 4] = fm(b_dw[l], 4)
        wd = np.asarray(w_dw[l], np.float32)
        consts[:, o + C_WDW:o + C_WDW + 124] = wd.T.reshape(4, 128, 31).transpose(1, 0, 2).reshape(128, 124)
    consts[:, C_GFIN:C_GFIN + 8] = fm(g_final, 8)
    sinks_b = np.ascontiguousarray(np.broadcast_to(np.asarray(sinks, np.float32).reshape(1, DEPTH * 8), (128, DEPTH * 8)))
    cosT, sinT, masks = _host_constants()

    shared = {
        "w_in": w_in_p, "w_pw2": f(w_pw2), "w_o": f(w_o), "w_up": f(w_up), "w_down": f(w_down),
        "w_gate": f(w_ple_gate), "w_ple": f(w_ple), "consts": consts, "sinks_b": sinks_b, "masks": masks,
        "cosT": cosT, "sinT": sinT,
    }
    in_maps = []
    for i in range(NCORES):
        ss = slice(NS * i, NS * (i + 1))
        xT = np.concatenate([x_prompt[i].T, x_sample[ss].reshape(NS * TS, D).T], axis=1)
        pT = np.concatenate([p_prompt[:, i].transpose(0, 2, 1), p_sample[:, ss].reshape(DEPTH, NS * TS, 256).transpose(0, 2, 1)], axis=2)
        ck = cache_k[:, ss].reshape(DEPTH, NS, 128, 128)
        cv = cache_v[:, ss].reshape(DEPTH, NS, 128, 128)
        sc = state_conv[:, ss]
        m = dict(shared)
        m.update({
            "xT": np.ascontiguousarray(xT), "pT": np.ascontiguousarray(pT),
            "kcT": np.ascontiguousarray(ck.transpose(0, 3, 1, 2).reshape(DEPTH, 128, NS * 128)),
            "kc_nat": np.ascontiguousarray(ck), "vc_nat": np.ascontiguousarray(cv),
            "scT": np.ascontiguousarray(sc.transpose(0, 3, 1, 2)), "sc_nat": np.ascontiguousarray(sc),
        })
        in_maps.append(m)
    if "nc" not in _CACHE:
        _CACHE["nc"] = build_program(N_LAYERS)
    nc = _CACHE["nc"]
    res = run_bass_kernel_spmd(nc, in_maps, core_ids=list(range(NCORES)))
    R = res.results
    y_prompt = np.stack([R[i]["yT"][:, :SEQ].T for i in range(NCORES)])
    y_sample = np.concatenate([R[i]["yT"][:, SEQ:].T.reshape(NS, TS, D) for i in range(NCORES)])
    kwp = np.stack([R[i]["kwp"].transpose(0, 2, 1).reshape(DEPTH, 128, 2, 64) for i in range(NCORES)], axis=1)
    vwp = np.stack([R[i]["vwp"].reshape(DEPTH, 128, 2, 64) for i in range(NCORES)], axis=1)
    cvp = np.stack([R[i]["cvp"].transpose(0, 2, 1) for i in range(NCORES)], axis=1)
    kws, vws, cvs = [], [], []
    for i in range(NCORES):
        kb = R[i]["kws_b"].reshape(DEPTH, 128, NS, TS).transpose(0, 2, 3, 1)
        kws.append(np.concatenate([R[i]["kws_a"], kb], axis=2).reshape(DEPTH, NS, 128, 2, 64))
        vb = R[i]["vws_b"].reshape(DEPTH, NS, TS, 128)
        vws.append(np.concatenate([R[i]["vws_a"], vb], axis=2).reshape(DEPTH, NS, 128, 2, 64))
        cb = R[i]["cvs_b"].reshape(DEPTH, 512, NS, TS).transpose(0, 2, 3, 1)
        cvs.append(np.concatenate([R[i]["cvs_a"], cb], axis=2))
    kws = np.concatenate(kws, axis=1)
    vws = np.concatenate(vws, axis=1)
    cvs = np.concatenate(cvs, axis=1)
    asf = lambda a: np.ascontiguousarray(a, dtype=np.float32)
    return (asf(y_prompt), asf(y_sample), asf(kwp), asf(vwp), asf(cvp), asf(kws), asf(vws), asf(cvs))
```

```python
import numpy as np
from contextlib import ExitStack
import concourse.bass as bass
import concourse.mybir as mybir
from concourse.bass_utils import run_bass_kernel_spmd

F32 = mybir.dt.float32
BF16 = mybir.dt.bfloat16
AF = mybir.ActivationFunctionType
ALU = mybir.AluOpType
AX = mybir.AxisListType

NCORES = 8
DEPTH = 4
D = 1024
SEQ = 2048
NS = 16
TS = 4
NTOK = SEQ + NS * TS
EPS = 1e-6
CONV_K = 31
HIST = 30
NSLOT = 3
SLOT_EL = 8192
TILES = [(0, 512), (512, 512), (1024, 512), (1536, 512), (2048, 64)]
C_GMIX, C_GFFN, C_GPLE, C_GAO, C_GCO, C_LNG, C_LNB, C_BDW, C_WDW = 0, 8, 16, 24, 28, 32, 36, 40, 44
C_PER = 44 + 4 * 31
C_GFIN = DEPTH * C_PER
NCONST = C_GFIN + 8
M_CUR, M_PREV, M_ID, M_SC, M_SN = 0, 128, 256, 384, 640
NMASK = 704


class Sched:
    def __init__(self, nc, es, n_io=24, n_wl=12):
        self.nc = nc
        self.eng = {"pe": nc.tensor, "act": nc.scalar, "dve": nc.vector, "pool": nc.gpsimd, "sp": nc.sync}
        self.sem = {}
        self.cnt = {}
        for name in self.eng:
            self.sem[name] = es.enter_context(nc.semaphore("s_" + name))
            self.cnt[name] = 0
        self.pools = {}
        for pname, n in (("io", n_io), ("wl", n_wl)):
            sems = [es.enter_context(nc.semaphore("s_%s%d" % (pname, i))) for i in range(n)]
            self.pools[pname] = {"sems": sems, "val": [0] * n, "next": 0}
        self.waited = {}
        self.last_w = {}
        self.readers = {}
        self.n_wait = 0
        self.n_ins = 0

    def _semh(self, semkey):
        if isinstance(semkey, str):
            return self.sem[semkey]
        return self.pools[semkey[0]]["sems"][semkey[1]]

    def _wait(self, engname, tok):
        semkey, val, _ = tok
        k = (engname, semkey)
        if self.waited.get(k, 0) >= val:
            return
        self.waited[k] = val
        self.eng[engname].wait_ge(self._semh(semkey), val)
        self.n_wait += 1

    def _deps(self, engname, reads, writes, is_dma):
        for key in reads:
            t = self.last_w.get(key)
            if t is not None:
                if (not is_dma) and t[2] == engname and engname == "pe":
                    continue
                self._wait(engname, t)
            if key == "psT" or (isinstance(key, tuple) and key[0] == "ps"):
                rd = self.readers.get(key)
                if rd:
                    for semkey, (val, src) in rd.items():
                        if src != engname:
                            self._wait(engname, (semkey, val, src))
        for key in writes:
            t = self.last_w.get(key)
            if t is not None:
                if is_dma or t[2] != engname or engname != "pe":
                    self._wait(engname, t)
            rd = self.readers.get(key)
            if rd:
                for semkey, (val, src) in rd.items():
                    if (not is_dma) and src == engname and engname == "pe":
                        continue
                    self._wait(engname, (semkey, val, src))

    def _commit(self, tok, reads, writes):
        for key in writes:
            self.last_w[key] = tok
            self.readers[key] = {}
        for key in reads:
            rd = self.readers.setdefault(key, {})
            old = rd.get(tok[0])
            if old is None or old[0] < tok[1]:
                rd[tok[0]] = (tok[1], tok[2])

    def op(self, engname, fn, reads=(), writes=()):
        self._deps(engname, reads, writes, False)
        ins = fn(self.eng[engname])
        self.cnt[engname] += 1
        ins.then_inc(self.sem[engname], 1)
        tok = (engname, self.cnt[engname], engname)
        self._commit(tok, reads, writes)
        self.n_ins += 1
        return tok

    def dma(self, engname, fns, reads=(), writes=(), pool="io"):
        P = self.pools[pool]
        slot = P["next"]
        P["next"] = (slot + 1) % len(P["sems"])
        if P["val"][slot] > 0:
            self._wait(engname, ((pool, slot), P["val"][slot], None))
        self._deps(engname, reads, writes, True)
        for fn in fns:
            ins = fn(self.eng[engname])
            ins.then_inc(P["sems"][slot], 16)
            P["val"][slot] += 16
            self.n_ins += 1
        tok = ((pool, slot), P["val"][slot], None)
        self._commit(tok, reads, writes)
        return tok

    def barrier(self, engines=("pe", "act", "dve", "pool", "sp")):
        for e in engines:
            for o in ("pe", "act", "dve", "pool"):
                if o != e and self.cnt[o] > 0:
                    self._wait(e, (o, self.cnt[o], o))
            P = self.pools["io"]
            for slot in range(len(P["sems"])):
                if P["val"][slot] > 0:
                    self._wait(e, (("io", slot), P["val"][slot], None))

    def wait_keys(self, engname, keys):
        for key in keys:
            t = self.last_w.get(key)
            if t is not None:
                self._wait(engname, t)


def build_program(n_layers=DEPTH):
    nc = bass.Bass("TRN2", target_bir_lowering=False)

    def din(name, shape):
        return nc.dram_tensor(name, list(shape), F32, kind="ExternalInput").ap()

    def dout(name, shape):
        return nc.dram_tensor(name, list(shape), F32, kind="ExternalOutput").ap()

    x_d = din("xT", [D, NTOK])
    p_d = din("pT", [DEPTH, 256, NTOK])
    kc_d = din("kcT", [DEPTH, 128, NS * 128])
    kcn_d = din("kc_nat", [DEPTH, NS, 128, 128])
    vc_d = din("vc_nat", [DEPTH, NS, 128, 128])
    sc_d = din("scT", [DEPTH, 512, NS, HIST])
    scn_d = din("sc_nat", [DEPTH, NS, HIST, 512])
    win_d = din("w_in", [DEPTH, D, 1792])
    wpw_d = din("w_pw2", [DEPTH, 512, 512])
    wo_d = din("w_o", [DEPTH, D, D])
    wup_d = din("w_up", [DEPTH, D, 4096])
    wdn_d = din("w_down", [DEPTH, 4096, D])
    wg_d = din("w_gate", [DEPTH, D, D])
    wpl_d = din("w_ple", [DEPTH, 256, D])
    const_d = din("consts", [128, NCONST])
    sink_d = din("sinks_b", [128, DEPTH * 8])
    mask_d = din("masks", [128, NMASK])
    cos_d = din("cosT", [128, NTOK])
    sin_d = din("sinT", [128, NTOK])

    y_d = dout("yT", [D, NTOK])
    kwp_d = dout("kwp", [DEPTH, 128, 128])
    vwp_d = dout("vwp", [DEPTH, 128, 128])
    cvp_d = dout("cvp", [DEPTH, 512, HIST])
    kwsa_d = dout("kws_a", [DEPTH, NS, 124, 128])
    kwsb_d = dout("kws_b", [DEPTH, 128, NS * TS])
    vwsa_d = dout("vws_a", [DEPTH, NS, 124, 128])
    vwsb_d = dout("vws_b", [DEPTH, NS * TS, 128])
    cvsa_d = dout("cvs_a", [DEPTH, NS, 26, 512])
    cvsb_d = dout("cvs_b", [DEPTH, 512, NS * TS])
    out_keys = []

    with ExitStack() as es:
        S = Sched(nc, es)

        uniq = [0]

        def sb(stack, name, shape, dt):
            uniq[0] += 1
            return stack.enter_context(nc.sbuf_tensor("%s_%d" % (name, uniq[0]), list(shape), dt))

        xT = sb(es, "xTs", [128, 8, NTOK], F32)
        slots = [sb(es, "slot%d" % i, [128, SLOT_EL], BF16) for i in range(NSLOT)]
        wple = [sb(es, "wple0", [128, 2, D], BF16)] * 2
        consts = sb(es, "constS", [128, NCONST], F32)
        esink = sb(es, "esink", [128, DEPTH * 8], F32)
        masks = sb(es, "maskS", [128, NMASK], BF16)
        ones = sb(es, "ones", [128, 128], BF16)
        epst = sb(es, "epst", [128, 1], F32)
        psb = [es.enter_context(nc.psum_tensor("psb%d" % i, [128, 512], F32)) for i in range(8)]
        psT = None
        ps_rr = [0]
        conv_cnt = [0]

        def next_ps():
            b = ps_rr[0]
            ps_rr[0] = (b + 1) % 7
            return b

        xv = x_d.rearrange("(c p) t -> p c t", p=128)
        def load_x(ti):
            t0, n = TILES[ti]
            S.dma("sp", [lambda e: e.dma_start(out=xT[:, :, t0:t0 + n], in_=xv[:, :, t0:t0 + n])],
                  writes=[("x", ti, c) for c in range(8)])

        S.dma("sp", [lambda e: e.dma_start(out=consts[:, :], in_=const_d)], writes=["consts"])
        load_x(0)
        S.dma("sp", [lambda e: e.dma_start(out=esink[:, :], in_=sink_d)], writes=["esink"])
        S.dma("pool", [lambda e: e.dma_start(out=masks[:, :], in_=mask_d)], writes=["masks"], pool="wl")
        for ti in range(1, len(TILES)):
            load_x(ti)
        S.op("dve", lambda e: e.memset(ones[:, :], 1.0 / 1024.0), writes=["ones"])
        S.op("dve", lambda e: e.memset(epst[:, :], EPS), writes=["epst"])
        S.op("act", lambda e: e.activation(out=esink[:, :], in_=esink[:, :], func=AF.Exp), reads=["esink"], writes=["esink"])

        pieces = []
        for l in range(n_layers):
            winv = win_d[l].rearrange("(c p) n -> p c n", p=128)
            pieces.append((("in_b", l), [
                (lambda s: s[:, :].rearrange("p (c n) -> p c n", c=8), winv[:, :, 768:1792]),
            ]))
            pieces.append((("in_a", l), [
                (lambda s: s[:, 0:6144].rearrange("p (c n) -> p c n", c=8), winv[:, :, 0:768]),
                (lambda s: s[:, 6144:8192].rearrange("p (c n) -> p c n", c=4), wpw_d[l].rearrange("(c p) n -> p c n", p=128)),
            ]))
            pieces.append((("o", l), [
                (lambda s: s[:, :].rearrange("p (c n) -> p c n", c=8), wo_d[l].rearrange("(c p) n -> p c n", p=128)),
            ]))
            for g in range(8):
                pieces.append((("ffn", l, g), [
                    (lambda s: s[:, 0:4096].rearrange("p (c n) -> p c n", c=8),
                     wup_d[l].rearrange("(c p) n -> p c n", p=128)[:, :, g * 512:(g + 1) * 512]),
                    (lambda s: s[:, 4096:8192].rearrange("p (c n) -> p c n", c=4),
                     wdn_d[l, g * 512:(g + 1) * 512, :].rearrange("(c p) n -> p c n", p=128)),
                ]))
            pieces.append((("gate", l), [
                (lambda s: s[:, :].rearrange("p (c n) -> p c n", c=8), wg_d[l].rearrange("(c p) n -> p c n", p=128)),
            ]))
        piece_slot = {}
        wstate = {"next": 0}

        def issue_next_piece(force=None):
            if force is None:
                i = wstate["next"]
                if i >= len(pieces):
                    return
                wstate["next"] = i + 1
            else:
                i = force
            name, parts = pieces[i]
            si = i % NSLOT
            piece_slot[name] = si
            fns = [(lambda e, d=dst, s_=src: e.dma_start(out=d(slots[si]), in_=s_)) for dst, src in parts]
            S.dma("pool", fns, writes=[("slot", si)], pool="wl")

        def load_wple(l):
            S.dma("pool", [lambda e: e.dma_start(out=wple[l % 2][:, :, :], in_=wpl_d[l].rearrange("(c p) n -> p c n", p=128))],
                  writes=[("wple", 0)], pool="wl")

        for i_ in (1, 0, 2):
            issue_next_piece(force=i_)
        wstate["next"] = NSLOT
        load_wple(0)

        def cc(l, base, j=0):
            col = l * C_PER + base + j
            return consts[:, col:col + 1]

        def rmsnorm_tile(l_gcol, ti, sq, rstd, out_fn, out_keys_fn, final=False):
            t0, n = TILES[ti]
            b = next_ps()
            for half in range(2):
                S.op("act", lambda e: e.activation(out=sq[:, :, 0:n], in_=xT[:, half * 4:half * 4 + 4, t0:t0 + n], func=AF.Square),
                     reads=[("x", ti, c) for c in range(half * 4, half * 4 + 4)], writes=["sq"])
                for c in range(4):
                    S.op("pe", lambda e, c=c: e.matmul(psb[b][:, 0:n], lhsT=ones[:, :], rhs=sq[:, c, 0:n],
                                                        start=(half == 0 and c == 0), stop=(half == 1 and c == 3)),
                         reads=["ones", "sq"], writes=[("ps", b)])
            S.op("act", lambda e: e.activation(out=rstd[:, 0:n], in_=psb[b][:, 0:n], func=AF.Ln, bias=epst[:, :], scale=1.0),
                 reads=[("ps", b), "epst"], writes=["rstd"])
            S.op("act", lambda e: e.activation(out=rstd[:, 0:n], in_=rstd[:, 0:n], func=AF.Exp, scale=-0.5), reads=["rstd"], writes=["rstd"])
            for c in range(8):
                S.op("dve", lambda e, c=c: e.scalar_tensor_tensor(out=out_fn(c), in0=xT[:, c, t0:t0 + n],
                                                                   scalar=consts[:, l_gcol + c:l_gcol + c + 1],
                                                                   in1=rstd[:, 0:n], op0=ALU.mult, op1=ALU.mult),
                     reads=[("x", ti, c), "rstd", "consts"], writes=out_keys_fn(c))

        def mm_group(b, n, lhs_fn, rhs_fn, nk, reads):
            for k in range(nk):
                S.op("pe", lambda e, k=k: e.matmul(psb[b][:, 0:n], lhsT=lhs_fn(k), rhs=rhs_fn(k), start=(k == 0), stop=(k == nk - 1)),
                     reads=reads, writes=[("ps", b)])

        for l in range(n_layers):
            with ExitStack() as pa:
                sq = sb(pa, "sq", [128, 4, 512], BF16)
                rstd = sb(pa, "rstd", [128, 512], F32)
                ft = [sb(pa, "ft%d" % i, [128, 512], F32) for i in range(4)]
                bufX = sb(pa, "bufX", [128, 8, 512], BF16)
                bufY = sb(pa, "bufY", [128, 4, 512], BF16)
                cst = sb(pa, "cst", [128, 512], F32)
                snt = sb(pa, "snt", [128, 512], F32)
                vf32 = sb(pa, "vf32", [128, 128], F32)
                Eball = sb(pa, "Eball", [128, 1024], BF16)
                Eb = [Eball[:, 0:512], Eball[:, 512:1024]]
                att_a = sb(pa, "att_a", [128, 8, 64], F32)
                att_n = sb(pa, "att_n", [128, 512], BF16)
                att_s = sb(pa, "att_s", [128, 16], F32)
                cacc = sb(pa, "cacc", [128, 4, 512], F32)
                ubuf = sb(pa, "ubuf", [128, 4 * NS * 34], BF16)
                ukeep = sb(pa, "ukeep", [128, 4, 64], F32)
                dgb = [sb(pa, "dgb%d" % i, [128, CONV_K, 128], BF16) for i in range(2)]
                kT = sb(pa, "kT", [128, 2048], BF16)
                vtok = sb(pa, "vtok", [128, 16, 2, 65], BF16)
                ksn = sb(pa, "ksn", [128, 64], BF16)
                vsn = sb(pa, "vsn", [64, 2, 65], BF16)
                Es = Eball
                Esn = sb(pa, "Esn", [64, 128], BF16)
                hT = sb(pa, "hTb", [128, 8, 512], BF16)
                mixT = bufX
                q_out = bufY
                sT = sb(pa, "sTb", [128, 4, 512], BF16)
                tA = att_a[:, :, :].rearrange("p h d -> p (h d)")
                tB = Eball[:, :].bitcast(F32)
                uP = ubuf[:, 0:4 * (HIST + 512)].rearrange("p (c t) -> p c t", c=4)
                uS = ubuf[:, 0:4 * NS * 34].rearrange("p (c s t) -> p c s t", c=4, s=NS)

                si_a = lambda: slots[piece_slot[("in_a", l)]]
                si_b = lambda: slots[piece_slot[("in_b", l)]]
                si_o = lambda: slots[piece_slot[("o", l)]]
                key_a = lambda: ("slot", piece_slot[("in_a", l)])
                key_b = lambda: ("slot", piece_slot[("in_b", l)])
                key_o = lambda: ("slot", piece_slot[("o", l)])
                wina = lambda: si_a()[:, 0:6144].rearrange("p (c n) -> p c n", c=8)
                wpw = lambda: si_a()[:, 6144:8192].rearrange("p (c n) -> p c n", c=4)
                winb = lambda: si_b()[:, :].rearrange("p (c n) -> p c n", c=8)
                wo = lambda: si_o()[:, :].rearrange("p (c n) -> p c n", c=8)

                S.op("dve", lambda e: e.memset(vtok[:, :, :, 64:65], 1.0), writes=["vtok_ones"])
                S.op("dve", lambda e: e.memset(vsn[:, :, 64:65], 1.0), writes=["vsn_ones"])
                S.op("dve", lambda e: e.memset(uP[:, :, 0:HIST], 0.0), writes=[("u", c) for c in range(4)])

                def emit_A0(ti):
                    t0, n = TILES[ti]
                    S.dma("sp", [lambda e: e.dma_start(out=cst[:, 0:n], in_=cos_d[:, t0:t0 + n]),
                                 lambda e: e.dma_start(out=snt[:, 0:n], in_=sin_d[:, t0:t0 + n])], writes=["cs"])
                    rmsnorm_tile(l * C_PER + C_GMIX, ti, sq, rstd, lambda c: hT[:, c, 0:n], lambda c: [("hx", c)])

                def gen_A(ti):
                    t0, n = TILES[ti]
                    sample = (ti == 4)
                    nblk = n // 128 if not sample else 0
                    if sample:
                        S.dma("pool", [lambda e: e.dma_start(out=kT[:, :], in_=kc_d[l])], writes=["kT", "kT_prev", "kT_cur"], pool="wl")
                        S.dma("pool", [(lambda e, k_=k_: e.dma_start(out=vtok[:, :, k_, 0:64],
                                                              in_=vc_d[l].rearrange("s p (k d) -> p s k d", k=2)[:, :, k_, :])) for k_ in range(2)],
                              reads=["vtok_ones"], writes=["vtok", "vtok_prev"] + [("vtok", b_) for b_ in range(1, 5)], pool="wl")
                        S.dma("pool", [(lambda e, c_=c_: e.dma_start(out=uS[:, c_, :, 0:HIST],
                                                            in_=sc_d[l].rearrange("(c p) s j -> p c s j", p=128)[:, c_, :, :])) for c_ in range(4)],
                              writes=[("u", c) for c in range(4)], pool="wl")
                        k1 = ("o_kwsa", l); k2 = ("o_vwsa", l); k3 = ("o_cvsa", l)
                        S.dma("sp", [lambda e: e.dma_start(out=kwsa_d[l], in_=kcn_d[l, :, 4:128, :])], writes=[k1])
                        S.dma("sp", [lambda e: e.dma_start(out=vwsa_d[l], in_=vc_d[l, :, 4:128, :])], writes=[k2])
                        S.dma("sp", [lambda e: e.dma_start(out=cvsa_d[l], in_=scn_d[l, :, 4:HIST, :])], writes=[k3])
                        out_keys.extend([k1, k2, k3])
                    hkeys = [("hx", c) for c in range(8)]

                    def rope(b, dst_ap, dst_keys):
                        A_, B_ = ft[0], ft[1]
                        S.op("dve", lambda e: e.tensor_tensor(out=A_[:, 0:n], in0=psb[b][:, 0:n], in1=cst[:, 0:n], op=ALU.mult),
                             reads=[("ps", b), "cs"], writes=["ft0"])
                        for (dst, src) in ((0, 32), (32, 0), (64, 96), (96, 64)):
                            S.op("dve", lambda e, dst=dst, src=src: e.tensor_tensor(out=B_[dst:dst + 32, 0:n], in0=psb[b][src:src + 32, 0:n],
                                                                                 in1=snt[src:src + 32, 0:n], op=ALU.mult),
                                 reads=[("ps", b), "cs"], writes=["ft1"])
                        S.op("dve", lambda e: e.tensor_tensor(out=dst_ap, in0=A_[:, 0:n], in1=B_[:, 0:n], op=ALU.add),
                             reads=["ft0", "ft1"], writes=dst_keys)

                    for c in range(4):
                        b = next_ps()
                        mm_group(b, n, lambda k, c=c: wina()[:, k, c * 128:(c + 1) * 128], lambda k: hT[:, k, 0:n], 8, hkeys + [key_a()])
                        rope(b, q_out[:, c, 0:n], [("by", c)])
                        yield
                    b = next_ps()
                    mm_group(b, n, lambda k: wina()[:, k, 512:640], lambda k: hT[:, k, 0:n], 8, hkeys + [key_a()])
                    rope(b, ft[3][:, 0:n], ["ft3"])
                    yield
                    if not sample:
                        S.op("act", lambda e: e.activation(out=kT[:, 128:128 + n], in_=ft[3][:, 0:n], func=AF.Copy),
                             reads=["ft3"], writes=["kT_cur"])
                        if ti == 3:
                            ko = ("o_kwp", l)
                            S.dma("sp", [lambda e: e.dma_start(out=kwp_d[l], in_=ft[3][:, 384:512])], reads=["ft3"], writes=[ko])
                            out_keys.append(ko)
                    else:
                        S.op("act", lambda e: e.activation(out=ksn[:, :], in_=ft[3][:, 0:64], func=AF.Copy),
                             reads=["ft3"], writes=["ksn"])
                        ko = ("o_kwsb", l)
                        S.dma("sp", [lambda e: e.dma_start(out=kwsb_d[l], in_=ft[3][:, 0:64])], reads=["ft3"], writes=[ko])
                        out_keys.append(ko)
                    if not sample:
                        for blk in range(nblk):
                            b = next_ps()
                            mm_group(b, 128, lambda k, blk=blk: hT[:, k, blk * 128:(blk + 1) * 128], lambda k: wina()[:, k, 640:768], 8,
                                     hkeys + [key_a()])
                            S.op("act", lambda e, blk=blk, b=b: e.activation(out=vtok[:, 1 + blk, :, 0:64],
                                                                              in_=psb[b][:, 0:128].rearrange("p (k d) -> p k d", k=2), func=AF.Copy),
                                 reads=[("ps", b), "vtok_ones"], writes=[("vtok", 1 + blk)])
                            if ti == 3 and blk == 3:
                                S.op("dve", lambda e, b=b: e.tensor_copy(out=vf32[:, :], in_=psb[b][:, 0:128]), reads=[("ps", b)], writes=["vf32"])
                                vo = ("o_vwp", l)
                                S.dma("sp", [lambda e: e.dma_start(out=vwp_d[l], in_=vf32[:, :])], reads=["vf32"], writes=[vo])
                                out_keys.append(vo)
                    else:
                        b = next_ps()
                        for k in range(8):
                            S.op("pe", lambda e, k=k: e.matmul(psb[b][:, 0:128], lhsT=hT[:, k, 0:128], rhs=wina()[:, k, 640:768],
                                                                start=(k == 0), stop=(k == 7)),
                                 reads=hkeys + [key_a()], writes=[("ps", b)])
                        S.op("act", lambda e: e.activation(out=vsn[:, :, 0:64], in_=psb[b][0:64, 0:128].rearrange("p (k d) -> p k d", k=2), func=AF.Copy),
                             reads=[("ps", b), "vsn_ones"], writes=["vsn"])
                        S.op("dve", lambda e: e.tensor_copy(out=vf32[0:64, :], in_=psb[b][0:64, 0:128]), reads=[("ps", b)], writes=["vf32"])
                        vo = ("o_vwsb", l)
                        S.dma("sp", [lambda e: e.dma_start(out=vwsb_d[l], in_=vf32[0:64, :])], reads=["vf32"], writes=[vo])
                        out_keys.append(vo)
                    for c in range(4):
                        ba = next_ps()
                        mm_group(ba, n, lambda k, c=c: winb()[:, k, c * 128:(c + 1) * 128], lambda k: hT[:, k, 0:n], 8, hkeys + [key_b()])
                        bg = next_ps()
                        mm_group(bg, n, lambda k, c=c: winb()[:, k, 512 + c * 128:512 + (c + 1) * 128], lambda k: hT[:, k, 0:n], 8, hkeys + [key_b()])
                        S.op("act", lambda e: e.activation(out=ft[2][:, 0:n], in_=psb[bg][:, 0:n], func=AF.Sigmoid),
                             reads=[("ps", bg)], writes=["ft2"])
                        if not sample:
                            S.op("dve", lambda e, c=c: e.tensor_tensor(out=uP[:, c, HIST:HIST + n], in0=psb[ba][:, 0:n], in1=ft[2][:, 0:n], op=ALU.mult),
                                 reads=[("ps", ba), "ft2"], writes=[("u", c)])
                            if ti == 3:
                                S.op("dve", lambda e, c=c: e.tensor_tensor(out=ukeep[:, c, 0:HIST], in0=psb[ba][:, 482:512], in1=ft[2][:, 482:512], op=ALU.mult),
                                     reads=[("ps", ba), "ft2"], writes=[("ukeep", c)])
                        else:
                            S.op("dve", lambda e, c=c: e.tensor_tensor(out=ukeep[:, c, 0:64], in0=psb[ba][:, 0:64], in1=ft[2][:, 0:64], op=ALU.mult),
                                 reads=[("ps", ba), "ft2"], writes=[("ukeep", c)])
                            S.op("dve", lambda e, c=c: e.tensor_tensor(out=uS[:, c, :, HIST:HIST + TS],
                                                                        in0=psb[ba][:, 0:n].rearrange("p (s t) -> p s t", s=NS),
                                                                        in1=ft[2][:, 0:n].rearrange("p (s t) -> p s t", s=NS), op=ALU.mult),
                                 reads=[("ps", ba), "ft2"], writes=[("u", c)])
                    if sample:
                        issue_next_piece()
                    if ti == 3:
                        co = ("o_cvp", l)
                        S.dma("sp", [lambda e: e.dma_start(out=cvp_d[l].rearrange("(c p) j -> p c j", p=128), in_=ukeep[:, :, 0:HIST])],
                              reads=[("ukeep", c) for c in range(4)], writes=[co])
                        out_keys.append(co)
                    if sample:
                        co = ("o_cvsb", l)
                        S.dma("sp", [lambda e: e.dma_start(out=cvsb_d[l].rearrange("(c p) t -> p c t", p=128), in_=ukeep[:, :, 0:64])],
                              reads=[("ukeep", c) for c in range(4)], writes=[co])
                        out_keys.append(co)

                    yield

                def do_B(ti):
                    t0, n = TILES[ti]
                    sample = (ti == 4)
                    nblk = n // 128 if not sample else 0
                    CB = 7

                    def conv_build(c):
                        gi = conv_cnt[0]
                        conv_cnt[0] += 1
                        par = gi % 2
                        base = l * C_PER + C_WDW + c * 31
                        S.op("pool", lambda e: e.tensor_tensor(
                            out=dgb[par][:, :, :], in0=masks[:, M_ID:M_ID + 128].unsqueeze(1).broadcast_to([128, CONV_K, 128]),
                            in1=consts[:, base:base + CONV_K].unsqueeze(2).broadcast_to([128, CONV_K, 128]), op=ALU.mult),
                             reads=["masks", "consts"], writes=[("dg", par)])
                        return par

                    def conv_taps(c, par, j0, j1):
                        for j in range(j0, j1):
                            if not sample:
                                S.op("pe", lambda e, j=j: e.matmul(psb[CB][:, 0:n], lhsT=dgb[par][:, j, :], rhs=uP[:, c, j:j + n],
                                                                   start=(j == 0), stop=(j == CONV_K - 1)),
                                     reads=[("dg", par), ("u", c)], writes=[("ps", CB)])
                            else:
                                S.op("pe", lambda e, j=j: e.matmul(psb[CB][:, 0:n].rearrange("p (s t) -> p s t", s=NS), lhsT=dgb[par][:, j, :],
                                                                   rhs=uS[:, c, :, j:j + TS], start=(j == 0), stop=(j == CONV_K - 1)),
                                     reads=[("dg", par), ("u", c)], writes=[("ps", CB)])
                        if j1 == CONV_K:
                            S.op("act", lambda e: e.activation(out=cacc[:, c, 0:n], in_=psb[CB][:, 0:n], func=AF.Identity, bias=cc(l, C_BDW, c), scale=1.0),
                                 reads=[("ps", CB), "consts"], writes=[("cacc", c)])

                    cpar = [conv_build(c_) for c_ in range(2)] + [None, None]

                    es8 = esink[:, l * 8:(l + 1) * 8]
                    if not sample:
                        for blk in range(nblk):
                            has_prev = not (ti == 0 and blk == 0)
                            bo = []
                            for kvh in range(2):
                                pr = slice(kvh * 64, (kvh + 1) * 64)
                                qrhs = q_out[pr, :, blk * 128:(blk + 1) * 128]
                                parts = ([("prev", blk * 128, M_PREV, blk)] if has_prev else []) + [("cur", 128 + blk * 128, M_CUR, blk + 1)]
                                ebufs = []
                                for pi, (nm, kcol, mcol, vblk) in enumerate(parts):
                                    b = next_ps()
                                    kkey = "kT_cur" if nm == "cur" or blk > 0 else "kT_prev"
                                    S.op("pe", lambda e, b=b, kcol=kcol: e.matmul(psb[b][:, :].rearrange("p (g q) -> p g q", g=4),
                                                                                   lhsT=kT[pr, kcol:kcol + 128], rhs=qrhs, start=True, stop=True),
                                         reads=[kkey] + [("by", c) for c in range(4)], writes=[("ps", b)])
                                    Ebuf = Eb[pi]
                                    S.op("act", lambda e, b=b, Ebuf=Ebuf: e.activation(out=Ebuf[:, :], in_=psb[b][:, :], func=AF.Exp, scale=0.125),
                                         reads=[("ps", b)], writes=[("Eb", pi)])
                                    S.op("dve", lambda e, Ebuf=Ebuf, mcol=mcol: e.tensor_tensor(
                                        out=Ebuf[:, :].rearrange("p (g q) -> p g q", g=4), in0=Ebuf[:, :].rearrange("p (g q) -> p g q", g=4),
                                        in1=masks[:, mcol:mcol + 128].unsqueeze(1).broadcast_to([128, 4, 128]), op=ALU.mult),
                                         reads=[("Eb", pi), "masks"], writes=[("Eb", pi)])
                                    ebufs.append((pi, vblk))
                                conv_taps(blk, cpar[blk], kvh * 8, kvh * 8 + 8)
                                b = next_ps()
                                bo.append(b)
                                for g in range(4):
                                    for ii, (pi, vblk) in enumerate(ebufs):
                                        vkey = ("vtok", vblk) if vblk > 0 else "vtok_prev"
                                        S.op("pe", lambda e, b=b, g=g, pi=pi, vblk=vblk, ii=ii: e.matmul(
                                            psb[b][:, g * 65:(g + 1) * 65], lhsT=Eb[pi][:, g * 128:(g + 1) * 128], rhs=vtok[:, vblk, kvh, :],
                                            start=(ii == 0), stop=(ii == len(ebufs) - 1)),
                                             reads=[("Eb", pi), vkey, "vtok_ones"], writes=[("ps", b)])
                            conv_taps(blk, cpar[blk], 16, CONV_K)
                            if blk + 2 < 4:
                                cpar[blk + 2] = conv_build(blk + 2)
                            for kvh in range(2):
                                b = bo[kvh]
                                pv = psb[b][:, 0:260].rearrange("p (g d) -> p g d", g=4)
                                S.op("dve", lambda e, pv=pv, kvh=kvh: e.tensor_tensor(out=att_s[:, kvh * 4:(kvh + 1) * 4], in0=pv[:, :, 64],
                                                                                      in1=es8[:, kvh * 4:(kvh + 1) * 4], op=ALU.add),
                                     reads=[("ps", b), "esink"], writes=[("att_s", kvh)])
                                S.op("dve", lambda e, kvh=kvh: e.reciprocal(out=att_s[:, kvh * 4:(kvh + 1) * 4], in_=att_s[:, kvh * 4:(kvh + 1) * 4]),
                                     reads=[("att_s", kvh)], writes=[("att_s", kvh)])
                                S.op("dve", lambda e, pv=pv, kvh=kvh: e.tensor_tensor(
                                    out=att_a[:, kvh * 4:(kvh + 1) * 4, :], in0=pv[:, :, 0:64],
                                    in1=att_s[:, kvh * 4:(kvh + 1) * 4].unsqueeze(2).broadcast_to([128, 4, 64]), op=ALU.mult),
                                     reads=[("ps", b), ("att_s", kvh)], writes=[("att_a", kvh)])
                            attn_finish(S, l, n_rows=128, att_a=att_a, att_n=att_n, att_s=att_s, ft=ft, epst=epst, masks=masks, psT=(lambda b_: (psb[b_], ("ps", b_)))(next_ps()),
                                        consts=consts, mix_dst=lambda c, blk=blk: mixT[:, c, blk * 128:(blk + 1) * 128], par=blk % 2)
                            if blk == 1 and ti + 1 < len(TILES):
                                emit_A0(ti + 1)
                        if ti < 3:
                            S.op("act", lambda e: e.activation(out=kT[:, 0:128], in_=kT[:, 512:640], func=AF.Copy),
                                 reads=["kT_cur"], writes=["kT_prev"])
                            S.op("act", lambda e: e.activation(out=vtok[:, 0, :, 0:64], in_=vtok[:, 4, :, 0:64], func=AF.Copy),
                                 reads=[("vtok", 4)], writes=["vtok_prev"])
                    else:
                        for c in range(4):
                            conv_taps(c, cpar[c], 0, CONV_K)
                            if c + 2 < 4:
                                cpar[c + 2] = conv_build(c + 2)
                        for hf in range(2):
                            bo = []
                            for kvh in range(2):
                                pr = slice(kvh * 64, (kvh + 1) * 64)
                                qrhs = q_out[pr, :, hf * 32:(hf + 1) * 32]
                                bcs = []
                                for half2 in range(2):
                                    b = next_ps()
                                    bcs.append(b)
                                    for sq_ in range(4):
                                        s_ = hf * 8 + half2 * 4 + sq_
                                        S.op("pe", lambda e, b=b, sq_=sq_, s_=s_: e.matmul(
                                            psb[b][:, sq_ * 128:(sq_ + 1) * 128].rearrange("p (g q) -> p g q", g=4),
                                            lhsT=kT[pr, s_ * 128:(s_ + 1) * 128], rhs=qrhs, start=True, stop=True),
                                             reads=["kT"] + [("by", c) for c in range(4)], writes=[("ps", b)])
                                    S.op("act", lambda e, b=b, half2=half2: e.activation(out=Es[:, half2 * 512:(half2 + 1) * 512], in_=psb[b][:, :],
                                                                                       func=AF.Exp, scale=0.125),
                                         reads=[("ps", b)], writes=[("Eb", half2)])
                                    S.op("dve", lambda e, half2=half2: e.tensor_tensor(
                                        out=Es[:, half2 * 512:(half2 + 1) * 512].rearrange("p (a g q) -> p a g q", a=4, g=4),
                                        in0=Es[:, half2 * 512:(half2 + 1) * 512].rearrange("p (a g q) -> p a g q", a=4, g=4),
                                        in1=masks[:, M_SC + half2 * 128:M_SC + (half2 + 1) * 128].rearrange("p (a q) -> p a q", a=4)
                                        .unsqueeze(2).broadcast_to([128, 4, 4, 32]), op=ALU.mult),
                                         reads=[("Eb", half2), "masks"], writes=[("Eb", half2)])
                                b = next_ps()
                                S.op("pe", lambda e, b=b: e.matmul(psb[b][0:64, 0:128].rearrange("p (g q) -> p g q", g=4),
                                                                    lhsT=ksn[pr, :], rhs=qrhs, start=True, stop=True),
                                     reads=["ksn"] + [("by", c) for c in range(4)], writes=[("ps", b)])
                                S.op("act", lambda e, b=b: e.activation(out=Esn[:, :], in_=psb[b][0:64, 0:128], func=AF.Exp, scale=0.125),
                                     reads=[("ps", b)], writes=["Esn"])
                                S.op("dve", lambda e: e.tensor_tensor(
                                    out=Esn[:, :].rearrange("p (g q) -> p g q", g=4), in0=Esn[:, :].rearrange("p (g q) -> p g q", g=4),
                                    in1=masks[0:64, M_SN + hf * 32:M_SN + (hf + 1) * 32].unsqueeze(1).broadcast_to([64, 4, 32]), op=ALU.mult),
                                     reads=["Esn", "masks"], writes=["Esn"])
                                b = next_ps()
                                bo.append(b)
                                for sq_ in range(8):
                                    s_ = hf * 8 + sq_
                                    S.op("pe", lambda e, b=b, sq_=sq_, s_=s_: e.matmul(
                                        psb[b][:, 0:65], lhsT=Es[:, sq_ * 128:(sq_ + 1) * 128], rhs=vtok[:, s_, kvh, :],
                                        start=(sq_ == 0), stop=False),
                                         reads=[("Eb", sq_ // 4), "vtok", "vtok_ones"], writes=[("ps", b)])
                                S.op("pe", lambda e, b=b: e.matmul(psb[b][:, 0:65], lhsT=Esn[:, :], rhs=vsn[:, kvh, :], start=False, stop=True),
                                     reads=["Esn", "vsn", "vsn_ones"], writes=[("ps", b)])
                            for kvh in range(2):
                                b = bo[kvh]
                                for g in range(4):
                                    h = kvh * 4 + g
                                    rows_src = slice(g * 32, (g + 1) * 32)
                                    rows_dst = slice(hf * 32, (hf + 1) * 32)
                                    S.op("dve", lambda e, b=b, h=h, rows_src=rows_src: e.tensor_tensor(
                                        out=ft[3][rows_src, h:h + 1], in0=psb[b][rows_src, 64:65], in1=es8[rows_src, h:h + 1], op=ALU.add),
                                         reads=[("ps", b), "esink"], writes=["ft3"])
                                    S.op("dve", lambda e, h=h, rows_src=rows_src: e.reciprocal(out=ft[3][rows_src, h:h + 1], in_=ft[3][rows_src, h:h + 1]),
                                         reads=["ft3"], writes=["ft3"])
                                    S.op("dve", lambda e, b=b, h=h, rows_src=rows_src: e.tensor_scalar(
                                        out=ft[0][rows_src, h * 64:(h + 1) * 64], in0=psb[b][rows_src, 0:64], scalar1=ft[3][rows_src, h:h + 1],
                                        scalar2=None, op0=ALU.mult),
                                         reads=[("ps", b), "ft3"], writes=["ft0"])
                                    S.op("dve", lambda e, h=h, rows_src=rows_src, rows_dst=rows_dst: e.tensor_copy(
                                        out=att_a[rows_dst, h, :], in_=ft[0][rows_src, h * 64:(h + 1) * 64]),
                                         reads=["ft0"], writes=[("att_a", kvh)])
                        attn_finish(S, l, n_rows=64, att_a=att_a, att_n=att_n, att_s=att_s, ft=ft, epst=epst, masks=masks, psT=(lambda b_: (psb[b_], ("ps", b_)))(next_ps()),
                                    consts=consts, mix_dst=lambda c: mixT[:, c, 0:64], par=0)

                    if ti < 3:
                        S.op("act", lambda e: e.activation(out=uP[:, :, 0:HIST], in_=uP[:, :, 512:542], func=AF.Copy),
                             reads=[("u", c) for c in range(4)], writes=[("u", c) for c in range(4)])
                    if ti == 3:
                        pass

                def gen_C(ti):
                    t0, n = TILES[ti]
                    sample = (ti == 4)
                    nblk = n // 128 if not sample else 0
                    ckeys = [("cacc", c) for c in range(4)]
                    S.op("act", lambda e: e.activation(out=sq[:, :, 0:n], in_=cacc[:, :, 0:n], func=AF.Copy), reads=ckeys, writes=["sq"])
                    b1 = next_ps()
                    mm_group(b1, n, lambda k: ones[:, :], lambda k: sq[:, k, 0:n], 4, ["ones", "sq"])
                    S.op("act", lambda e: e.activation(out=tA[:, 0:n], in_=psb[b1][:, 0:n], func=AF.Identity, scale=2.0),
                         reads=[("ps", b1)], writes=["c_mean"])
                    S.op("act", lambda e: e.activation(out=sq[:, :, 0:n], in_=cacc[:, :, 0:n], func=AF.Square), reads=ckeys, writes=["sq"])
                    b2 = next_ps()
                    mm_group(b2, n, lambda k: ones[:, :], lambda k: sq[:, k, 0:n], 4, ["ones", "sq"])
                    S.op("dve", lambda e: e.tensor_tensor(out=tB[:, 0:n], in0=tA[:, 0:n], in1=tA[:, 0:n], op=ALU.mult),
                         reads=["c_mean"], writes=["c_rstd"])
                    S.op("dve", lambda e: e.scalar_tensor_tensor(out=tB[:, 0:n], in0=psb[b2][:, 0:n], scalar=2.0, in1=tB[:, 0:n],
                                                                 op0=ALU.mult, op1=ALU.subtract),
                         reads=[("ps", b2), "c_rstd"], writes=["c_rstd"])
                    S.op("dve", lambda e: e.tensor_scalar_max(out=tB[:, 0:n], in0=tB[:, 0:n], scalar1=0.0), reads=["c_rstd"], writes=["c_rstd"])
                    S.op("act", lambda e: e.activation(out=tB[:, 0:n], in_=tB[:, 0:n], func=AF.Ln, bias=epst[:, :], scale=1.0),
                         reads=["c_rstd", "epst"], writes=["c_rstd"])
                    S.op("act", lambda e: e.activation(out=tB[:, 0:n], in_=tB[:, 0:n], func=AF.Exp, scale=-0.5), reads=["c_rstd"], writes=["c_rstd"])
                    yield
                    for c in range(4):
                        S.op("dve", lambda e, c=c: e.tensor_tensor(out=ft[2][:, 0:n], in0=cacc[:, c, 0:n], in1=tA[:, 0:n], op=ALU.subtract),
                             reads=[("cacc", c), "c_mean"], writes=["ft2"])
                        S.op("dve", lambda e: e.tensor_tensor(out=ft[2][:, 0:n], in0=ft[2][:, 0:n], in1=tB[:, 0:n], op=ALU.mult),
                             reads=["ft2", "c_rstd"], writes=["ft2"])
                        S.op("act", lambda e, c=c: e.activation(out=sT[:, c, 0:n], in_=ft[2][:, 0:n], func=AF.Silu,
                                                                 bias=cc(l, C_LNB, c), scale=cc(l, C_LNG, c)),
                             reads=["ft2", "consts"], writes=[("sy", c)])
                        yield
                    for oc in range(4):
                        b = next_ps()
                        mm_group(b, n, lambda k, oc=oc: wpw()[:, k, oc * 128:(oc + 1) * 128], lambda k: sT[:, k, 0:n], 4,
                                 [("sy", c) for c in range(4)] + [key_a()])
                        S.op("act", lambda e, oc=oc, b=b: e.activation(out=cacc[:, oc, 0:n], in_=psb[b][:, 0:n], func=AF.Copy),
                             reads=[("ps", b)], writes=[("cacc", oc)])
                        S.op("act", lambda e, oc=oc, b=b: e.activation(out=sq[:, oc, 0:n], in_=psb[b][:, 0:n], func=AF.Square),
                             reads=[("ps", b)], writes=["sq"])
                        yield
                    b3 = next_ps()
                    mm_group(b3, n, lambda k: ones[:, :], lambda k: sq[:, k, 0:n], 4, ["ones", "sq"])
                    S.op("act", lambda e: e.activation(out=rstd[:, 0:n], in_=psb[b3][:, 0:n], func=AF.Ln, bias=epst[:, :], scale=2.0),
                         reads=[("ps", b3), "epst"], writes=["rstd"])
                    S.op("act", lambda e: e.activation(out=rstd[:, 0:n], in_=rstd[:, 0:n], func=AF.Exp, scale=-0.5), reads=["rstd"], writes=["rstd"])
                    for oc in range(4):
                        S.op("dve", lambda e, oc=oc: e.scalar_tensor_tensor(out=mixT[:, 4 + oc, 0:n], in0=cacc[:, oc, 0:n], scalar=cc(l, C_GCO, oc),
                                                                            in1=rstd[:, 0:n], op0=ALU.mult, op1=ALU.mult),
                             reads=[("cacc", oc), "rstd", "consts"], writes=[("bx", 4 + oc)])
                    for oc in range(8):
                        b = next_ps()
                        mm_group(b, n, lambda k, oc=oc: wo()[:, k, oc * 128:(oc + 1) * 128], lambda k: mixT[:, k, 0:n], 8,
                                 [("bx", c) for c in range(8)] + [key_o()])
                        S.op("dve", lambda e, oc=oc, b=b: e.tensor_tensor(out=xT[:, oc, t0:t0 + n], in0=psb[b][:, 0:n], in1=xT[:, oc, t0:t0 + n], op=ALU.add),
                             reads=[("ps", b), ("x", ti, oc)], writes=[("x", ti, oc)])
                        yield
                    yield

                def drain(g):
                    for _ in g:
                        pass

                def zipgens(ga, gb, ratio=1):
                    da = db = False
                    while not (da and db):
                        if not da:
                            try:
                                next(ga)
                            except StopIteration:
                                da = True
                        if not db:
                            try:
                                next(gb)
                            except StopIteration:
                                db = True

                emit_A0(0)
                drain(gen_A(0))
                for ti in range(len(TILES)):
                    do_B(ti)
                    if ti + 1 < len(TILES):
                        zipgens(gen_A(ti + 1), gen_C(ti))
                    else:
                        drain(gen_C(ti))
                issue_next_piece()
                issue_next_piece()
                S.barrier()
            with ExitStack() as pe_:
                h2 = sb(pe_, "h2", [128, 8, NTOK], BF16)
                sq = sb(pe_, "sqE", [128, 4, 512], BF16)
                rstd = sb(pe_, "rstdE", [128, 512], F32)
                hid = [sb(pe_, "hid%d" % i, [128, 4, 512], BF16) for i in range(2)]
                rl = [sb(pe_, "rl%d" % i, [128, 512], F32) for i in range(2)]
                def h2_norm(tj):
                    tj0, nj = TILES[tj]
                    rmsnorm_tile(l * C_PER + C_GFFN, tj, sq, rstd, lambda c: h2[:, c, tj0:tj0 + nj], lambda c: [("h2", tj, c)])

                items = [(g, ti) for g in range(8) for ti in range(len(TILES))]

                def ffn_w(g):
                    sl = slots[piece_slot[("ffn", l, g)]]
                    return (sl[:, 0:4096].rearrange("p (c n) -> p c n", c=8), sl[:, 4096:8192].rearrange("p (c n) -> p c n", c=4),
                            ("slot", piece_slot[("ffn", l, g)]))

                def ffn_up(idx):
                    g, ti = items[idx]
                    t0, n = TILES[ti]
                    wup, _, skey = ffn_w(g)
                    hb = hid[idx % 2]
                    for hc in range(4):
                        b = next_ps()
                        mm_group(b, n, lambda k, hc=hc: wup[:, k, hc * 128:(hc + 1) * 128], lambda k: h2[:, k, t0:t0 + n], 8,
                                 [("h2", ti, c) for c in range(8)] + [skey])
                        r = rl[hc % 2]
                        S.op("act", lambda e, b=b, r=r: e.activation(out=r[:, 0:n], in_=psb[b][:, 0:n], func=AF.Relu),
                             reads=[("ps", b)], writes=[("rl", hc % 2)])
                        S.op("pool", lambda e, r=r, hc=hc: e.tensor_tensor(out=hb[:, hc, 0:n], in0=r[:, 0:n], in1=r[:, 0:n], op=ALU.mult),
                             reads=[("rl", hc % 2)], writes=[("hid", idx % 2, hc)])

                def ffn_down(idx):
                    g, ti = items[idx]
                    t0, n = TILES[ti]
                    _, wdn, skey = ffn_w(g)
                    hb = hid[idx % 2]
                    for oc in range(8):
                        b = next_ps()
                        mm_group(b, n, lambda k, oc=oc: wdn[:, k, oc * 128:(oc + 1) * 128], lambda k: hb[:, k, 0:n], 4,
                                 [("hid", idx % 2, c) for c in range(4)] + [skey])
                        S.op("dve", lambda e, oc=oc, b=b: e.tensor_tensor(out=xT[:, oc, t0:t0 + n], in0=psb[b][:, 0:n], in1=xT[:, oc, t0:t0 + n], op=ALU.add),
                             reads=[("ps", b), ("x", ti, oc)], writes=[("x", ti, oc)])

                h2_norm(0)
                h2_norm(1)
                ffn_up(0)
                for idx in range(len(items)):
                    g, ti = items[idx]
                    if idx + 1 < len(items):
                        g1, t1 = items[idx + 1]
                        if g1 == 0 and t1 + 1 < len(TILES):
                            h2_norm(t1 + 1)
                        ffn_up(idx + 1)
                    ffn_down(idx)
                    if ti == len(TILES) - 1:
                        issue_next_piece()
                S.barrier()
            with ExitStack() as pf:
                sq = sb(pf, "sqF", [128, 4, 512], BF16)
                rstd = sb(pf, "rstdF", [128, 512], F32)
                hTF = [sb(pf, "hTF%d" % i, [128, 8, 512], BF16) for i in range(2)]
                pT = [sb(pf, "pTF%d" % i, [128, 2, 512], BF16) for i in range(2)]
                sg = [sb(pf, "sgF%d" % i, [128, 512], F32) for i in range(2)]
                tmpf = [sb(pf, "tmpF%d" % i, [128, 512], F32) for i in range(2)]
                yt = sb(pf, "ytF", [128, 8, 512], F32)
                sl = lambda: slots[piece_slot[("gate", l)]]
                skey = lambda: ("slot", piece_slot[("gate", l)])
                wg = lambda: sl()[:, :].rearrange("p (c n) -> p c n", c=8)
                wp = wple[l % 2]
                def f_pre(tj):
                    tj0, nj = TILES[tj]
                    pbj = pT[tj % 2]
                    hj = hTF[tj % 2]
                    S.dma("pool", [lambda e: e.dma_start(out=pbj[:, :, 0:nj], in_=p_d[l].rearrange("(c p) t -> p c t", p=128)[:, :, tj0:tj0 + nj])],
                          writes=[("pT", tj % 2)], pool="wl")
                    rmsnorm_tile(l * C_PER + C_GPLE, tj, sq, rstd, lambda c: hj[:, c, 0:nj], lambda c: [("hTF", tj % 2, c)])

                f_pre(0)
                for ti, (t0, n) in enumerate(TILES):
                    pb = pT[ti % 2]
                    hT = hTF[ti % 2]
                    if ti + 1 < len(TILES):
                        f_pre(ti + 1)
                    for oc in range(8):
                        bg = next_ps()
                        mm_group(bg, n, lambda k, oc=oc: wg()[:, k, oc * 128:(oc + 1) * 128], lambda k: hT[:, k, 0:n], 8,
                                 [("hTF", ti % 2, c) for c in range(8)] + [skey()])
                        bp = next_ps()
                        mm_group(bp, n, lambda k, oc=oc: wp[:, k, oc * 128:(oc + 1) * 128], lambda k: pb[:, k, 0:n], 2,
                                 [("pT", ti % 2), ("wple", 0)])
                        s_ = sg[oc % 2]
                        t_ = tmpf[oc % 2]
                        S.op("act", lambda e, s_=s_, bg=bg: e.activation(out=s_[:, 0:n], in_=psb[bg][:, 0:n], func=AF.Sigmoid),
                             reads=[("ps", bg)], writes=[("sgF", oc % 2)])
                        S.op("dve", lambda e, s_=s_, t_=t_, bp=bp: e.tensor_tensor(out=t_[:, 0:n], in0=psb[bp][:, 0:n], in1=s_[:, 0:n], op=ALU.mult),
                             reads=[("ps", bp), ("sgF", oc % 2)], writes=[("tmpF", oc % 2)])
                        S.op("pool", lambda e, t_=t_, oc=oc: e.tensor_tensor(out=xT[:, oc, t0:t0 + n], in0=t_[:, 0:n], in1=xT[:, oc, t0:t0 + n], op=ALU.add),
                             reads=[("tmpF", oc % 2), ("x", ti, oc)], writes=[("x", ti, oc)])
                    if l == n_layers - 1:
                        rmsnorm_tile(C_GFIN, ti, sq, rstd, lambda c: yt[:, c, 0:n], lambda c: [("yt", c)])
                        yo = ("o_y", ti)
                        S.dma("sp", [lambda e: e.dma_start(out=y_d.rearrange("(c p) t -> p c t", p=128)[:, :, t0:t0 + n], in_=yt[:, :, 0:n])],
                              reads=[("yt", c) for c in range(8)], writes=[yo])
                        out_keys.append(yo)
                issue_next_piece()
                if l + 1 < n_layers:
                    load_wple(l + 1)
                S.barrier()
        S.wait_keys("sp", out_keys)
        S.barrier(engines=("sp",))
    return nc


def attn_finish(S, l, n_rows, att_a, att_n, att_s, ft, epst, masks, psT, consts, mix_dst, par):
    R = slice(0, n_rows)
    a2 = att_a[R, :, :].rearrange("p h d -> p (h d)")
    S.op("act", lambda e: e.activation(out=ft[2][R, :], in_=a2, func=AF.Square), reads=[("att_a", 0), ("att_a", 1)], writes=["ft2"])
    S.op("dve", lambda e: e.reduce_sum(out=att_s[R, 8:9], in_=ft[2][R, :], axis=AX.X), reads=["ft2"], writes=["att_ss"])
    S.op("act", lambda e: e.activation(out=att_s[R, 8:9], in_=att_s[R, 8:9], func=AF.Ln, bias=epst[R, :], scale=1.0 / 512.0),
         reads=["att_ss", "epst"], writes=["att_ss"])
    S.op("act", lambda e: e.activation(out=att_s[R, 8:9], in_=att_s[R, 8:9], func=AF.Exp, scale=-0.5), reads=["att_ss"], writes=["att_ss"])
    S.op("act", lambda e: e.activation(out=att_n[R, :], in_=a2, func=AF.Identity, scale=att_s[R, 8:9]),
         reads=[("att_a", 0), ("att_a", 1), "att_ss"], writes=["att_n"])
    pb_, pkey = psT
    pv_ = pb_[:, 0:256].bitcast(BF16)
    for c in range(4):
        pt = pv_[:, c * 128: c * 128 + n_rows]
        S.op("pe", lambda e, c=c, pt=pt: e.transpose(pt, att_n[R, c * 128:(c + 1) * 128], masks[R, M_ID:M_ID + n_rows]),
             reads=["att_n", "masks"], writes=[pkey])
    for c in range(4):
        pt = pv_[:, c * 128: c * 128 + n_rows]
        S.op("act", lambda e, c=c, pt=pt: e.activation(out=mix_dst(c), in_=pt, func=AF.Identity,
                                                        scale=consts[:, l * C_PER + C_GAO + c:l * C_PER + C_GAO + c + 1]),
             reads=[pkey, "consts"], writes=[("bx", c)])


def _host_constants():
    half = 32
    inv = 1.0 / (10000.0 ** (np.arange(half, dtype=np.float64) / half))
    pos = np.concatenate([np.arange(SEQ, dtype=np.float64), np.tile(8192.0 + np.arange(TS, dtype=np.float64), NS)])
    ang = pos[None, :] * inv[:, None]
    cos32 = np.cos(ang).astype(np.float32)
    sin32 = np.sin(ang).astype(np.float32)
    lane = np.arange(128)
    f = lane % 32
    sign = np.where((lane % 64) < 32, 1.0, -1.0).astype(np.float32)
    cosT = cos32[f, :]
    sinT = sin32[f, :] * sign[:, None]
    masks = np.zeros((128, NMASK), np.float32)
    li = np.arange(128)[:, None]
    qi = np.arange(128)[None, :]
    masks[:, M_CUR:M_CUR + 128] = (li <= qi)
    masks[:, M_PREV:M_PREV + 128] = (li >= qi)
    masks[:, M_ID:M_ID + 128] = np.eye(128)
    msc = np.zeros((128, 8, 8, 4), np.float32)
    for sq in range(8):
        for t in range(4):
            msc[:, sq, sq, t] = (np.arange(128) >= t)
    masks[:, M_SC:M_SC + 256] = msc.reshape(128, 256)
    msn = np.zeros((64, 2, 8, 4), np.float32)
    for sk in range(16):
        for tp in range(4):
            for t in range(4):
                if tp <= t:
                    msn[sk * 4 + tp, sk // 8, sk % 8, t] = 1.0
    masks[0:64, M_SN:M_SN + 64] = msn.reshape(64, 64)
    return np.ascontiguousarray(cosT), np.ascontiguousarray(sinT), masks


_CACHE = {}
N_LAYERS = DEPTH


def kernel(x_prompt, x_sample, p_prompt, p_sample, cache_k, cache_v, state_conv,
           g_mix, w_in, sinks, w_dw, b_dw, ln_g, ln_b, w_pw2, g_attn_out, g_conv_out, w_o,
           g_ffn, w_up, w_down, g_ple, w_ple_gate, w_ple, g_final):
    f = lambda a: np.ascontiguousarray(np.asarray(a, dtype=np.float32))
    x_prompt, x_sample, p_prompt, p_sample = f(x_prompt), f(x_sample), f(p_prompt), f(p_sample)
    cache_k, cache_v, state_conv = f(cache_k), f(cache_v), f(state_conv)
    w_in = f(w_in)
    qperm = np.concatenate([np.concatenate([np.arange(c * 64, (c + 1) * 64), np.arange((4 + c) * 64, (5 + c) * 64)]) for c in range(4)])
    colperm = np.concatenate([qperm, np.arange(512, 1792)])
    w_in_p = np.ascontiguousarray(w_in[:, :, colperm])
    consts = np.zeros((128, NCONST), np.float32)

    def fm(v, nchunk):
        return np.asarray(v, np.float32).reshape(nchunk, 128).T

    for l in range(DEPTH):
        o = l * C_PER
        consts[:, o + C_GMIX:o + C_GMIX + 8] = fm(g_mix[l], 8)
        consts[:, o + C_GFFN:o + C_GFFN + 8] = fm(g_ffn[l], 8)
        consts[:, o + C_GPLE:o + C_GPLE + 8] = fm(g_ple[l], 8)
        consts[:, o + C_GAO:o + C_GAO + 4] = fm(g_attn_out[l], 4)
        consts[:, o + C_GCO:o + C_GCO + 4] = fm(g_conv_out[l], 4)
        consts[:, o + C_LNG:o + C_LNG + 4] = fm(ln_g[l], 4)
        consts[:, o + C_LNB:o + C_LNB + 4] = fm(ln_b[l], 4)
        consts[:, o + C_BDW:o + C_BDW + 4] = fm(b_dw[l], 4)
        wd = np.asarray(w_dw[l], np.float32)
        consts[:, o + C_WDW:o + C_WDW + 124] = wd.T.reshape(4, 128, 31).transpose(1, 0, 2).reshape(128, 124)
    consts[:, C_GFIN:C_GFIN + 8] = fm(g_final, 8)
    sinks_b = np.ascontiguousarray(np.broadcast_to(np.asarray(sinks, np.float32).reshape(1, DEPTH * 8), (128, DEPTH * 8)))
    cosT, sinT, masks = _host_constants()

    shared = {
        "w_in": w_in_p, "w_pw2": f(w_pw2), "w_o": f(w_o), "w_up": f(w_up), "w_down": f(w_down),
        "w_gate": f(w_ple_gate), "w_ple": f(w_ple), "consts": consts, "sinks_b": sinks_b, "masks": masks,
        "cosT": cosT, "sinT": sinT,
    }
    in_maps = []
    for i in range(NCORES):
        ss = slice(NS * i, NS * (i + 1))
        xT = np.concatenate([x_prompt[i].T, x_sample[ss].reshape(NS * TS, D).T], axis=1)
        pT = np.concatenate([p_prompt[:, i].transpose(0, 2, 1), p_sample[:, ss].reshape(DEPTH, NS * TS, 256).transpose(0, 2, 1)], axis=2)
        ck = cache_k[:, ss].reshape(DEPTH, NS, 128, 128)
        cv = cache_v[:, ss].reshape(DEPTH, NS, 128, 128)
        sc = state_conv[:, ss]
        m = dict(shared)
        m.update({
            "xT": np.ascontiguousarray(xT), "pT": np.ascontiguousarray(pT),
            "kcT": np.ascontiguousarray(ck.transpose(0, 3, 1, 2).reshape(DEPTH, 128, NS * 128)),
            "kc_nat": np.ascontiguousarray(ck), "vc_nat": np.ascontiguousarray(cv),
            "scT": np.ascontiguousarray(sc.transpose(0, 3, 1, 2)), "sc_nat": np.ascontiguousarray(sc),
        })
        in_maps.append(m)
    if "nc" not in _CACHE:
        _CACHE["nc"] = build_program(N_LAYERS)
    nc = _CACHE["nc"]
    res = run_bass_kernel_spmd(nc, in_maps, core_ids=list(range(NCORES)))
    R = res.results
    y_prompt = np.stack([R[i]["yT"][:, :SEQ].T for i in range(NCORES)])
    y_sample = np.concatenate([R[i]["yT"][:, SEQ:].T.reshape(NS, TS, D) for i in range(NCORES)])
    kwp = np.stack([R[i]["kwp"].transpose(0, 2, 1).reshape(DEPTH, 128, 2, 64) for i in range(NCORES)], axis=1)
    vwp = np.stack([R[i]["vwp"].reshape(DEPTH, 128, 2, 64) for i in range(NCORES)], axis=1)
    cvp = np.stack([R[i]["cvp"].transpose(0, 2, 1) for i in range(NCORES)], axis=1)
    kws, vws, cvs = [], [], []
    for i in range(NCORES):
        kb = R[i]["kws_b"].reshape(DEPTH, 128, NS, TS).transpose(0, 2, 3, 1)
        kws.append(np.concatenate([R[i]["kws_a"], kb], axis=2).reshape(DEPTH, NS, 128, 2, 64))
        vb = R[i]["vws_b"].reshape(DEPTH, NS, TS, 128)
        vws.append(np.concatenate([R[i]["vws_a"], vb], axis=2).reshape(DEPTH, NS, 128, 2, 64))
        cb = R[i]["cvs_b"].reshape(DEPTH, 512, NS, TS).transpose(0, 2, 3, 1)
        cvs.append(np.concatenate([R[i]["cvs_a"], cb], axis=2))
    kws = np.concatenate(kws, axis=1)
    vws = np.concatenate(vws, axis=1)
    cvs = np.concatenate(cvs, axis=1)
    asf = lambda a: np.ascontiguousarray(a, dtype=np.float32)
    return (asf(y_prompt), asf(y_sample), asf(kwp), asf(vwp), asf(cvp), asf(kws), asf(vws), asf(cvs))
```
